# Optimizing a Trainium2 kernel written in Bass

```python
import math
import jax, jax.numpy as jnp
from jax import lax
import numpy as np

D_MODEL = 1024
BATCH = 8
SEQ = 4096
DEPTH = 4

CHUNK = 64
MEM_LEN = 256
N_MIXERS = 2
SSM_GROUP = 16
SSM_GROUPS = D_MODEL // SSM_GROUP
SSM_STATE = 64
SB_HEADS = 16
SB_HEAD_DIM = D_MODEL // SB_HEADS
Q_BLOCK = 128
XA_HEADS = 4
XA_HEAD_DIM = D_MODEL // XA_HEADS
FFN_DIM = ((8 * D_MODEL // 3 + 127) // 128) * 128
CONV_WIDTH = 3
LN_EPS = 1e-5
DN_ALPHA = (2.0 * DEPTH) ** 0.25
DN_BETA = (8.0 * DEPTH) ** -0.25
N_S5 = (DEPTH + 1) // 2
N_SB = DEPTH // 2

kernel_name = "hybrid_s5_stickbreaking_deepnorm_encoder"


def layer_norm(x, g, b):
    xf = x.astype(jnp.float32)
    mu = jnp.mean(xf, axis=-1, keepdims=True)
    var = jnp.mean(jnp.square(xf - mu), axis=-1, keepdims=True)
    return ((xf - mu) * lax.rsqrt(var + LN_EPS) * g + b).astype(x.dtype)


def deepnorm_residual(x, h, g, b):
    return layer_norm(DN_ALPHA * x + h, g, b)


def _complex_linear_combine(e1, e2):
    a1r, a1i, b1r, b1i = e1
    a2r, a2i, b2r, b2i = e2
    ar = a2r * a1r - a2i * a1i
    ai = a2r * a1i + a2i * a1r
    br = a2r * b1r - a2i * b1i + b2r
    bi = a2r * b1i + a2i * b1r + b2i
    return (ar, ai, br, bi)


def s5_mixer(x, w_in, a_re, a_im, log_step, b_re, b_im, c_re, c_im, d_skip, w_out):
    f32 = jnp.float32
    bsz, seq, _ = x.shape
    n_chunks = seq // CHUNK
    u = (x @ w_in).astype(f32)
    a_re = a_re.astype(f32); a_im = a_im.astype(f32)
    step = jnp.exp(log_step.astype(f32))[:, None]
    mag = jnp.exp(a_re * step)
    ang = a_im * step
    lam_r = mag * jnp.cos(ang)
    lam_i = mag * jnp.sin(ang)
    den = a_re * a_re + a_im * a_im
    nr = lam_r - 1.0
    ni = lam_i
    f_r = (nr * a_re + ni * a_im) / den
    f_i = (ni * a_re - nr * a_im) / den
    b_re = b_re.astype(f32); b_im = b_im.astype(f32)
    bb_r = f_r[..., None] * b_re - f_i[..., None] * b_im
    bb_i = f_r[..., None] * b_im + f_i[..., None] * b_re
    c_r = c_re.astype(f32); c_i = c_im.astype(f32)
    lam_seq_r = jnp.broadcast_to(lam_r, (CHUNK, 1, SSM_GROUPS, SSM_STATE))
    lam_seq_i = jnp.broadcast_to(lam_i, (CHUNK, 1, SSM_GROUPS, SSM_STATE))
    u_c = u.reshape(bsz, n_chunks, CHUNK, SSM_GROUPS, SSM_GROUP).transpose(1, 2, 0, 3, 4)

    def chunk_step(carry, u_t):
        h_r, h_i = carry
        bu_r = jnp.einsum('tbgc,gpc->tbgp', u_t, bb_r)
        bu_i = jnp.einsum('tbgc,gpc->tbgp', u_t, bb_i)
        bu_r = bu_r.at[0].add(lam_r * h_r - lam_i * h_i)
        bu_i = bu_i.at[0].add(lam_r * h_i + lam_i * h_r)
        _, _, s_r, s_i = lax.associative_scan(
            _complex_linear_combine, (lam_seq_r, lam_seq_i, bu_r, bu_i), axis=0)
        y = (jnp.einsum('tbgp,gcp->tbgc', s_r, c_r)
             - jnp.einsum('tbgp,gcp->tbgc', s_i, c_i))
        return (s_r[-1], s_i[-1]), y

    init = (jnp.zeros((bsz, SSM_GROUPS, SSM_STATE), f32),
            jnp.zeros((bsz, SSM_GROUPS, SSM_STATE), f32))
    _, ys = lax.scan(chunk_step, init, u_c)
    y = ys.transpose(2, 0, 1, 3, 4).reshape(bsz, seq, D_MODEL)
    y = y + d_skip.astype(f32) * u
    y = jax.nn.gelu(y).astype(x.dtype)
    val, gate = jnp.split(y @ w_out, 2, axis=-1)
    return val * jax.nn.sigmoid(gate)


def stick_breaking_attention(x, w_qkv, w_o):
    bsz, seq, _ = x.shape
    q, k, v = jnp.split(x @ w_qkv, 3, axis=-1)
    q = q.reshape(bsz, seq, SB_HEADS, SB_HEAD_DIM)
    k = k.reshape(bsz, seq, SB_HEADS, SB_HEAD_DIM)
    v = v.reshape(bsz, seq, SB_HEADS, SB_HEAD_DIM)
    scale = SB_HEAD_DIM ** -0.5
    outs = []
    for start in range(0, seq, Q_BLOCK):
        end = start + Q_BLOCK
        qb = q[:, start:end]
        kb = k[:, :end]
        vb = v[:, :end]
        z = jnp.einsum('bqhd,bkhd->bhqk', qb, kb).astype(jnp.float32) * scale
        t_idx = start + jnp.arange(Q_BLOCK)[:, None]
        s_idx = jnp.arange(end)[None, :]
        mask = s_idx < t_idx
        log_beta = jax.nn.log_sigmoid(z)
        log_1m = jnp.where(mask, jax.nn.log_sigmoid(-z), 0.0)
        between = lax.cumsum(log_1m, axis=3, reverse=True) - log_1m
        w = jnp.where(mask, jnp.exp(log_beta + between), 0.0)
        outs.append(jnp.einsum('bhqk,bkhd->bqhd', w.astype(vb.dtype), vb))
    o = jnp.concatenate(outs, axis=1).reshape(bsz, seq, D_MODEL)
    return o @ w_o


def memory_cross_attention(x, mem, w_q, w_kv, w_o):
    bsz, seq, _ = x.shape
    q = (x @ w_q).reshape(bsz, seq, XA_HEADS, XA_HEAD_DIM)
    k, v = jnp.split(mem @ w_kv, 2, axis=-1)
    k = k.reshape(bsz, MEM_LEN, XA_HEADS, XA_HEAD_DIM)
    v = v.reshape(bsz, MEM_LEN, XA_HEADS, XA_HEAD_DIM)
    s = jnp.einsum('bqhd,bkhd->bhqk', q, k).astype(jnp.float32) * (XA_HEAD_DIM ** -0.5)
    p = jax.nn.softmax(s, axis=-1).astype(v.dtype)
    o = jnp.einsum('bhqk,bkhd->bqhd', p, v).reshape(bsz, seq, D_MODEL)
    return o @ w_o


def conv_ffn(x, w_up, conv_w, conv_b, w_down):
    u = x @ w_up
    u = lax.conv_general_dilated(
        u, conv_w[:, None, :], window_strides=(1,), padding=[(CONV_WIDTH - 1, 0)],
        dimension_numbers=('NWC', 'WIO', 'NWC'), feature_group_count=2 * FFN_DIM) + conv_b
    a, g = jnp.split(u, 2, axis=-1)
    return (a * jax.nn.gelu(g)) @ w_down


def setup_inputs(seed: int = 0) -> dict:
    key = jax.random.key(seed)
    ks = iter(jax.random.split(key, 32))
    f32 = jnp.float32

    def nrm(shape, scale):
        return jax.random.normal(next(ks), shape, f32) * scale

    D = D_MODEL
    G, P = SSM_GROUPS, SSM_STATE
    x = nrm((BATCH, SEQ, D), 1.0)
    mem = nrm((BATCH, MEM_LEN, D), 1.0)
    s5_w_in = nrm((N_S5, D, D), D ** -0.5)
    s5_a_re = -0.5 + nrm((N_S5, G, P), 0.01)
    s5_a_im = math.pi * jnp.arange(P, dtype=f32) + nrm((N_S5, G, P), 0.01)
    s5_log_step = jax.random.uniform(next(ks), (N_S5, G), f32, math.log(1e-3), math.log(1e-1))
    s5_b_re = nrm((N_S5, G, P, SSM_GROUP), (2.0 * SSM_GROUP) ** -0.5)
    s5_b_im = nrm((N_S5, G, P, SSM_GROUP), (2.0 * SSM_GROUP) ** -0.5)
    s5_c_re = nrm((N_S5, G, SSM_GROUP, P), (2.0 * P) ** -0.5)
    s5_c_im = nrm((N_S5, G, SSM_GROUP, P), (2.0 * P) ** -0.5)
    s5_d = nrm((N_S5, D), 1.0)
    s5_w_out = nrm((N_S5, D, 2 * D), D ** -0.5 * DN_BETA)
    sb_w_qkv = nrm((N_SB, D, 3 * D), D ** -0.5)
    sb_w_o = nrm((N_SB, D, D), D ** -0.5 * DN_BETA)
    xa_w_q = nrm((DEPTH, D, D), D ** -0.5)
    xa_w_kv = nrm((DEPTH, D, 2 * D), D ** -0.5)
    xa_w_o = nrm((DEPTH, D, D), D ** -0.5 * DN_BETA)
    ffn_w_up = nrm((DEPTH, D, 2 * FFN_DIM), D ** -0.5)
    ffn_conv_w = nrm((DEPTH, CONV_WIDTH, 2 * FFN_DIM), CONV_WIDTH ** -0.5)
    ffn_conv_b = nrm((DEPTH, 2 * FFN_DIM), 0.02)
    ffn_w_down = nrm((DEPTH, FFN_DIM, D), FFN_DIM ** -0.5 * DN_BETA)
    ln_g = 1.0 + nrm((DEPTH, 3, D), 0.02)
    ln_b = nrm((DEPTH, 3, D), 0.02)
    return {"x": x, "mem": mem,
            "s5_w_in": s5_w_in, "s5_a_re": s5_a_re, "s5_a_im": s5_a_im,
            "s5_log_step": s5_log_step, "s5_b_re": s5_b_re, "s5_b_im": s5_b_im,
            "s5_c_re": s5_c_re, "s5_c_im": s5_c_im, "s5_d": s5_d, "s5_w_out": s5_w_out,
            "sb_w_qkv": sb_w_qkv, "sb_w_o": sb_w_o,
            "xa_w_q": xa_w_q, "xa_w_kv": xa_w_kv, "xa_w_o": xa_w_o,
            "ffn_w_up": ffn_w_up, "ffn_conv_w": ffn_conv_w, "ffn_conv_b": ffn_conv_b,
            "ffn_w_down": ffn_w_down, "ln_g": ln_g, "ln_b": ln_b}


def reference(x, mem, s5_w_in, s5_a_re, s5_a_im, s5_log_step, s5_b_re, s5_b_im,
              s5_c_re, s5_c_im, s5_d, s5_w_out, sb_w_qkv, sb_w_o,
              xa_w_q, xa_w_kv, xa_w_o, ffn_w_up, ffn_conv_w, ffn_conv_b, ffn_w_down,
              ln_g, ln_b):
    for i in range(DEPTH):
        j = i // N_MIXERS
        if i % N_MIXERS == 0:
            h = s5_mixer(x, s5_w_in[j], s5_a_re[j], s5_a_im[j], s5_log_step[j],
                         s5_b_re[j], s5_b_im[j], s5_c_re[j], s5_c_im[j], s5_d[j], s5_w_out[j])
        else:
            h = stick_breaking_attention(x, sb_w_qkv[j], sb_w_o[j])
        x = deepnorm_residual(x, h, ln_g[i, 0], ln_b[i, 0])
        h = memory_cross_attention(x, mem, xa_w_q[i], xa_w_kv[i], xa_w_o[i])
        x = deepnorm_residual(x, h, ln_g[i, 1], ln_b[i, 1])
        h = conv_ffn(x, ffn_w_up[i], ffn_conv_w[i], ffn_conv_b[i], ffn_w_down[i])
        x = deepnorm_residual(x, h, ln_g[i, 2], ln_b[i, 2])
    return x
```

```python
import numpy as np
import ml_dtypes
import concourse.bass as bass
import concourse.mybir as mybir
from concourse.bass_utils import run_bass_kernel_spmd

F32 = mybir.dt.float32
BF16 = mybir.dt.bfloat16
AF = mybir.ActivationFunctionType
ALU = mybir.AluOpType
AX = mybir.AxisListType

D = 1024
S = 4096
DEPTH = 4
MEM = 256
FF = 2816
NCH = D // 128
NSUB = S // 128
NTT = S // 512
LN_EPS = 1e-5
DN_ALPHA = (2.0 * DEPTH) ** 0.25


class Op:
    __slots__ = ("eng", "fn", "deps", "need_inc", "cnt", "is_dma", "dsem", "dval", "prev_dma")


class Prog:
    CENG = ("pe", "act", "dve", "pool")
    NDS = 8

    def __init__(self, nc):
        self.nc = nc
        self.h = {"pe": nc.tensor, "act": nc.scalar, "dve": nc.vector, "pool": nc.gpsimd, "sp": nc.sync}
        self.esem = {e: nc.semaphore("sem_" + e).__enter__() for e in self.CENG}
        self.dsem = {q: [nc.semaphore(f"dsem_{q}{i}").__enter__() for i in range(self.NDS)]
                     for q in ("sp", "act", "pool")}
        self.ecnt = {e: 0 for e in self.CENG}
        self.dcount = {q: 0 for q in self.dsem}
        self.dhist = {q: [] for q in self.dsem}
        self.dval = {q: [0] * self.NDS for q in self.dsem}
        self.known = {e: {} for e in self.h}
        self.ops = []
        self.lastw = {}
        self.readers = {}
        self.n_instr = 0

    def add(self, eng, fn, reads=(), writes=(), dma=False):
        op = Op()
        op.eng = eng; op.fn = fn; op.need_inc = False; op.cnt = None; op.is_dma = dma
        op.dsem = None; op.dval = None; op.prev_dma = None
        deps = {}
        for k in reads:
            w = self.lastw.get(k)
            if w is not None:
                deps[id(w)] = (w, "raw")
        for k in writes:
            w = self.lastw.get(k)
            if w is not None and id(w) not in deps:
                deps[id(w)] = (w, "waw")
            r = self.readers.get(k)
            if r:
                for o in r["eng"].values():
                    if id(o) not in deps:
                        deps[id(o)] = (o, "war")
                for o in r["dma"]:
                    if id(o) not in deps:
                        deps[id(o)] = (o, "war")
        for k in writes:
            self.lastw[k] = op
            self.readers[k] = {"eng": {}, "dma": []}
        for k in reads:
            r = self.readers.setdefault(k, {"eng": {}, "dma": []})
            if dma:
                r["dma"].append(op)
            else:
                r["eng"][eng] = op
        flt = []
        for (p, kind) in deps.values():
            if p is op:
                continue
            if not p.is_dma and not dma and p.eng == eng:
                if eng == "pe" or kind != "raw":
                    continue
            flt.append(p)
            if not p.is_dma:
                p.need_inc = True
        op.deps = flt
        if dma:
            q = eng
            k = self.dcount[q]; self.dcount[q] += 1
            slot = k % self.NDS
            op.dsem = self.dsem[q][slot]
            self.dval[q][slot] += 16
            op.dval = self.dval[q][slot]
        self.ops.append(op)
        return op

    def _wait(self, eng, sem, val):
        kn = self.known[eng]
        key = id(sem)
        if kn.get(key, 0) >= val:
            return
        kn[key] = val
        self.h[eng].wait_ge(sem, val)
        self.n_instr += 1

    def flush(self, barrier=True):
        last = {}
        for op in self.ops:
            if not op.is_dma:
                last[op.eng] = op
        if barrier:
            for op in last.values():
                op.need_inc = True
        for op in self.ops:
            if not op.is_dma and op.need_inc:
                self.ecnt[op.eng] += 1
                op.cnt = self.ecnt[op.eng]
        for op in self.ops:
            eng = op.eng
            if op.is_dma:
                if op.dval > 16:
                    self._wait(eng, op.dsem, op.dval - 16)
            for p in op.deps:
                if p.is_dma:
                    self._wait(eng, p.dsem, p.dval)
                else:
                    self._wait(eng, self.esem[p.eng], p.cnt)
            ins = op.fn()
            self.n_instr += 1
            if op.is_dma:
                ins.then_inc(op.dsem, 16)
            elif op.need_inc:
                ins.then_inc(self.esem[eng], 1)
        if barrier:
            for e in self.h:
                for e2 in self.CENG:
                    if e2 != e and self.ecnt[e2] > 0:
                        self._wait(e, self.esem[e2], self.ecnt[e2])
                for q in self.dsem:
                    for i in range(self.NDS):
                        if self.dval[q][i] > 0:
                            self._wait(e, self.dsem[q][i], self.dval[q][i])
        self.ops = []
        self.lastw = {}
        self.readers = {}

    def finish(self):
        self.flush(barrier=True)


class Arena:
    def __init__(self, nc, base=0, cap=204800):
        self.nc = nc; self.base = base; self.off = base; self.cap = cap; self.n = 0

    def reset(self):
        self.off = self.base

    def alloc(self, name, shape, dtype):
        esz = 4 if dtype == F32 else 2
        n = 1
        for s in shape[1:]:
            n *= s
        size = (n * esz + 63) // 64 * 64
        assert self.off + size <= self.cap, f"SBUF arena overflow at {name}: {self.off + size}"
        self.n += 1
        t = self.nc.alloc_sbuf_tensor_at(f"{name}_{self.n}", list(shape), dtype, offset=self.off)
        self.off += size
        return t


class Builder:
    def __init__(self, plan):
        self.plan = plan
        nc = bass.Bass("TRN2", target_bir_lowering=False)
        self.nc = nc
        self.P = Prog(nc)
        dt = nc.dram_tensor

        def ext(name, shape):
            return dt(name, list(shape), F32, kind="ExternalInput").ap()

        self.x_in = ext("x", [S, D])
        self.mem = ext("mem", [MEM, D])
        self._wshapes = dict([
            ("s5_w_in", [2, D, D]), ("s5_a_re", [2, 64, 64]), ("s5_a_im", [2, 64, 64]), ("s5_log_step", [2, 64]),
            ("s5_b_re", [2, 64, 64, 16]), ("s5_b_im", [2, 64, 64, 16]), ("s5_c_re", [2, 64, 16, 64]),
            ("s5_c_im", [2, 64, 16, 64]), ("s5_d", [2, D]), ("s5_w_out", [2, D, 2 * D]),
            ("sb_w_qkv", [2, D, 3 * D]), ("sb_w_o", [2, D, D]),
            ("xa_w_q", [4, D, D]), ("xa_w_kv", [4, D, 2 * D]), ("xa_w_o", [4, D, D]),
            ("ffn_w_up", [4, D, 2 * FF]), ("ffn_conv_w", [4, 3, 2 * FF]), ("ffn_conv_b", [4, 2 * FF]),
            ("ffn_w_down", [4, FF, D]), ("ln_g", [4, 3, D]), ("ln_b", [4, 3, D]),
        ])
        self._ext = ext

        class _W(dict):
            def __missing__(d, name):
                d[name] = self._ext(name, self._wshapes[name])
                return d[name]
        self.w = _W()
        self.c_ident = ext("c_ident", [128, 128])
        self.c_s5 = ext("c_s5", [128, 10, 128])
        self.c_tri = ext("c_tri", [128, 4, 128])
        self.y = dt("y", [S, D], F32, kind="ExternalOutput").ap()
        self.xs = dt("xs", [S, D], F32).ap()
        self.xT = [dt("xTa", [D, S], BF16).ap(), dt("xTb", [D, S], BF16).ap()]
        self.actT = dt("actT", [FF, S], BF16).ap()
        self.midT = dt("midT", [D, S], BF16).ap()
        self.qT = dt("qT", [D, S], BF16).ap()
        self.kT = dt("kT", [D, S], BF16).ap()
        self.vtok = dt("vtok", [S, D], BF16).ap()
        self.pst = nc.alloc_psum_tensor("pst", [128, 8, 512], F32)
        self.parena = Arena(nc, base=16640, cap=24832)
        self.ident = self.parena.alloc("ident", [128, 128], F32)
        self.memT = self.parena.alloc("memT", [128, NCH, MEM], BF16)
        self.tri = self.parena.alloc("tri", [128, 4, 128], BF16)
        self.A = Arena(nc, base=24832, cap=229312)

    WRES_BASE = 184256

    def wres_tile(self, kc, nout):
        self._wres_n = getattr(self, "_wres_n", 0) + 1
        return self.nc.alloc_sbuf_tensor_at(f"Wres_{self._wres_n}", [128, kc, nout], BF16, offset=self.WRES_BASE)

    def dma(self, q, out, in_, reads, writes, **kw):
        h = self.P.h[q]
        return self.P.add(q, lambda: h.dma_start(out=out, in_=in_, **kw), reads, writes, dma=True)

    def load_w(self, dst, src2d, kc_n, name, col0=0, ncol=None):
        v = src2d.rearrange("(c p) n -> p c n", p=128)
        ncol = ncol if ncol is not None else dst.shape[2]
        for c in range(kc_n):
            self.dma("pool", dst[:, c, 0:ncol], v[:, c, col0:col0 + ncol], [], [(name, c)])

    def bank(self, b, n=1):
        if n == 1:
            return self.pst[:, b, :]
        return self.pst[:, b:b + n, :]

    def phase_prep(self, xT_dst):
        P, nc, A = self.P, self.nc, self.A
        A.reset()
        A.cap = 229312
        self.dma("sp", self.ident[:], self.c_ident, [], ["ident"])
        self.dma("pool", self.tri[:], self.c_tri, [], ["tri"])
        memf = A.alloc("memf", [128, 2, D], F32)
        self.dma("sp", memf[:], self.mem.rearrange("(s p) d -> p s d", p=128), [], ["memf"])
        for mc in range(2):
            for half in range(2):
                b = (mc * 2 + half) % 2
                ps = self.pst[:, b, :]
                for c4 in range(4):
                    c = half * 4 + c4
                    P.add("pe", lambda ps=ps, c4=c4, c=c, mc=mc: nc.tensor.transpose(
                        ps[:, c4 * 128:(c4 + 1) * 128], memf[:, mc, c * 128:(c + 1) * 128], self.ident[:]),
                        ["memf", "ident"], [("ps", b)])
                P.add("act", lambda ps=ps, half=half, mc=mc: nc.scalar.activation(
                    out=self.memT[:, half * 4:(half + 1) * 4, mc * 128:(mc + 1) * 128],
                    in_=ps.rearrange("p (c t) -> p c t", c=4), func=AF.Identity),
                    [("ps", b)], ["memT"])
        xin = [A.alloc("xin", [128, 4, D], F32) for _ in range(2)]
        xTn = [A.alloc("xTn", [128, NCH, 512], BF16) for _ in range(2)]
        xv = self.x_in.rearrange("(t s p) d -> t p s d", p=128, s=4)
        xTv = xT_dst.rearrange("(c p) t -> p c t", p=128)
        for tt in range(NTT):
            bi = tt % 2
            self.dma("sp", xin[bi][:], xv[tt], [], [("xin", bi)])
            for s4 in range(4):
                si = tt * 4 + s4
                for half in range(2):
                    b = 2 + (si * 2 + half) % 4
                    ps = self.pst[:, b, :]
                    for c4 in range(4):
                        c = half * 4 + c4
                        P.add("pe", lambda ps=ps, c4=c4, c=c, bi=bi, s4=s4: nc.tensor.transpose(
                            ps[:, c4 * 128:(c4 + 1) * 128], xin[bi][:, s4, c * 128:(c + 1) * 128], self.ident[:]),
                            [("xin", bi), "ident"], [("ps", b)])
                    eng = "act" if half == 0 else "dve"
                    outap = xTn[bi][:, half * 4:(half + 1) * 4, s4 * 128:(s4 + 1) * 128]
                    inap = ps.rearrange("p (c t) -> p c t", c=4)
                    if eng == "act":
                        P.add("act", lambda outap=outap, inap=inap: nc.scalar.activation(out=outap, in_=inap, func=AF.Identity),
                              [("ps", b)], [("xTn", bi, s4, half)])
                    else:
                        P.add("dve", lambda outap=outap, inap=inap: nc.vector.tensor_copy(out=outap, in_=inap),
                              [("ps", b)], [("xTn", bi, s4, half)])
            self.dma("sp", xTv[:, :, tt * 512:(tt + 1) * 512], xTn[bi][:],
                     [("xTn", bi, s4, half) for s4 in range(4) for half in range(2)], [("xTd", tt)])
        P.flush()

    def phase_proj_ln(self, srcT, KC, W2d, glu, g_ap, b_ap, x_src, x_dst, xT_dst, Wpre=None):
        P, nc, A = self.P, self.nc, self.A
        A.reset()
        NOUT = 2 * D if glu else D
        if Wpre is not None:
            Wt = Wpre
            A.cap = self.WRES_BASE
        else:
            A.cap = 229312
            Wt = A.alloc("W", [128, KC, NOUT], BF16)
            self.load_w(Wt, W2d, KC, "W")
        gt = A.alloc("g", [128, D], F32)
        bt = A.alloc("b", [128, D], F32)
        self.dma("sp", gt[:], g_ap.partition_broadcast(128), [], ["g"])
        self.dma("sp", bt[:], b_ap.partition_broadcast(128), [], ["b"])
        src = [A.alloc("src", [128, KC, 512], BF16) for _ in range(2)]
        xin = [A.alloc("xin", [128, 4, D], F32) for _ in range(2)]
        tbuf = [A.alloc("tbuf", [128, D], F32) for _ in range(3)]
        nbuf = [A.alloc("nbuf", [128, D], F32) for _ in range(4)]
        xTn = [A.alloc("xTn", [128, NCH, 512], BF16) for _ in range(2)]
        st = [A.alloc("st", [128, 16], F32) for _ in range(3)]
        nm = [A.alloc("nm", [128, 2], F32) for _ in range(3)]
        if glu:
            sg = [A.alloc("sg", [128, 512], F32) for _ in range(2)]
            hb = [A.alloc("hb", [128, D], F32) for _ in range(2)]
        srcv = srcT.rearrange("(c p) t -> p c t", p=128)
        xsv = x_src.rearrange("(t s p) d -> t p s d", p=128, s=4)
        xdv = x_dst.rearrange("(n p) d -> n p d", p=128)
        xTv = xT_dst.rearrange("(c p) t -> p c t", p=128) if xT_dst is not None else None
        nhb = 1 if glu else 2
        nbk = 4 if glu else 2

        def emit_tr(si):
            tt, s4 = divmod(si, 4)
            bi = tt % 2
            nb = nbuf[si % 4]
            for half in range(2):
                b = 6 + half
                ps = self.pst[:, b, :]
                for c4 in range(4):
                    c = half * 4 + c4
                    P.add("pe", lambda ps=ps, c4=c4, c=c, nb=nb: nc.tensor.transpose(
                        ps[:, c4 * 128:(c4 + 1) * 128], nb[:, c * 128:(c + 1) * 128], self.ident[:]),
                        [("nbuf", si % 4), "ident"], [("ps", b)])
                outap = xTn[bi][:, half * 4:(half + 1) * 4, s4 * 128:(s4 + 1) * 128]
                inap = ps.rearrange("p (c t) -> p c t", c=4)
                P.add("act", lambda outap=outap, inap=inap: nc.scalar.activation(out=outap, in_=inap, func=AF.Identity),
                      [("ps", b)], [("xTn", bi, s4, half)])
            if s4 == 3:
                self.dma("sp", xTv[:, :, tt * 512:(tt + 1) * 512], xTn[bi][:],
                         [("xTn", bi, a, hh) for a in range(4) for hh in range(2)], [("xTd", tt)])

        def loads(tt):
            if tt >= NTT:
                return
            bi = tt % 2
            self.dma("sp", src[bi][:], srcv[:, :, tt * 512:(tt + 1) * 512], [], [("src", bi)])
            self.dma("sp", xin[bi][:], xsv[tt], [], [("xin", bi)])

        loads(0)

        def stageA(si):
            tt, s4 = divmod(si, 4)
            bi = tt % 2
            if s4 == 1:
                loads(tt + 1)
            if glu:
                hbt = hb[si % 2]
                for hc in range(2):
                    set_ = (2 * si + hc) % 3
                    bv, bg = 2 * set_, 2 * set_ + 1
                    for (bb_, c0_) in ((bv, hc * 512), (bg, D + hc * 512)):
                        ps = self.pst[:, bb_, :]
                        for kc in range(KC):
                            P.add("pe", lambda ps=ps, kc=kc, c0_=c0_, bi=bi, s4=s4: nc.tensor.matmul(
                                ps, src[bi][:, kc, s4 * 128:(s4 + 1) * 128], Wt[:, kc, c0_:c0_ + 512],
                                start=(kc == 0), stop=(kc == KC - 1)),
                                [("src", bi), ("W", kc)], [("ps", bb_)])
                    sgt = sg[(2 * si + hc) % 2]
                    sgk = ("sg", (2 * si + hc) % 2)
                    P.add("act", lambda sgt=sgt, bg=bg: nc.scalar.activation(out=sgt[:], in_=self.pst[:, bg, :], func=AF.Exp, scale=-1.0),
                          [("ps", bg)], [sgk])
                    P.add("dve", lambda sgt=sgt: nc.vector.tensor_scalar(out=sgt[:], in0=sgt[:], scalar1=1.0, scalar2=None, op0=ALU.add),
                          [sgk], [sgk])
                    P.add("dve", lambda sgt=sgt: nc.vector.reciprocal(out=sgt[:], in_=sgt[:]), [sgk], [sgk])
                    P.add("dve", lambda sgt=sgt, bv=bv, hbt=hbt, hc=hc: nc.vector.tensor_tensor(
                        out=hbt[:, hc * 512:(hc + 1) * 512], in0=self.pst[:, bv, :], in1=sgt[:], op=ALU.mult),
                        [("ps", bv), sgk], [("hb", si % 2, hc)])
                hsrc = hbt[:]
                hkeys = [("hb", si % 2, 0), ("hb", si % 2, 1)]
            else:
                hbi = si % 2
                b0 = hbi * 2
                for half in range(2):
                    ps = self.pst[:, b0 + half, :]
                    for kc in range(KC):
                        P.add("pe", lambda ps=ps, kc=kc, half=half, bi=bi, s4=s4: nc.tensor.matmul(
                            ps, src[bi][:, kc, s4 * 128:(s4 + 1) * 128], Wt[:, kc, half * 512:(half + 1) * 512],
                            start=(kc == 0), stop=(kc == KC - 1)),
                            [("src", bi), ("W", kc)], [("ps", b0 + half)])
                hsrc = self.pst[:, b0:b0 + 2, :].rearrange("p a n -> p (a n)")
                hkeys = [("ps", b0), ("ps", b0 + 1)]
            tb = tbuf[si % 3]
            stt = st[si % 3]
            tk = ("tbuf", si % 3)
            sk = ("st", si % 3)
            P.add("dve", lambda tb=tb, bi=bi, s4=s4, hsrc=hsrc: nc.vector.scalar_tensor_tensor(
                out=tb[:], in0=xin[bi][:, s4, :], scalar=DN_ALPHA, in1=hsrc, op0=ALU.mult, op1=ALU.add),
                [("xin", bi)] + hkeys, [tk])
            P.add("dve", lambda tb=tb, stt=stt: nc.vector.bn_stats(out=stt[:, 0:6], in_=tb[:, 0:512]), [tk], [(sk, 0)])
            P.add("dve", lambda tb=tb, stt=stt: nc.vector.bn_stats(out=stt[:, 6:12], in_=tb[:, 512:1024]), [tk], [(sk, 1)])
            P.add("dve", lambda stt=stt: nc.vector.bn_aggr(out=stt[:, 12:14], in_=stt[:, 0:12].rearrange("p (a b) -> p a b", a=2)),
                  [(sk, 0), (sk, 1)], [(sk, 2)])
            P.add("act", lambda stt=stt: nc.scalar.activation(out=stt[:, 14:15], in_=stt[:, 13:14], func=AF.Ln, bias=LN_EPS),
                  [(sk, 2)], [(sk, 3)])
            P.add("act", lambda stt=stt: nc.scalar.activation(out=stt[:, 15:16], in_=stt[:, 14:15], func=AF.Exp, scale=-0.5),
                  [(sk, 3)], [(sk, 4)])

        def stageB(si):
            tb = tbuf[si % 3]
            stt = st[si % 3]
            nmm = nm[si % 3]
            nb = nbuf[si % 4]
            tk = ("tbuf", si % 3)
            sk = ("st", si % 3)
            nk = ("nbuf", si % 4)
            P.add("dve", lambda stt=stt, nmm=nmm: nc.vector.scalar_tensor_tensor(
                out=nmm[:, 0:1], in0=stt[:, 12:13], scalar=-1.0, in1=stt[:, 15:16], op0=ALU.mult, op1=ALU.mult),
                [(sk, 2), (sk, 4)], [("nm", si % 3)])
            P.add("act", lambda nb=nb, tb=tb, stt=stt, nmm=nmm: nc.scalar.activation(
                out=nb[:], in_=tb[:], func=AF.Identity, scale=stt[:, 15:16], bias=nmm[:, 0:1]),
                [tk, (sk, 4), ("nm", si % 3)], [nk])
            P.add("dve", lambda nb=nb: nc.vector.tensor_tensor(out=nb[:], in0=nb[:], in1=gt[:], op=ALU.mult), [nk, "g"], [nk])
            P.add("pool", lambda nb=nb: nc.gpsimd.tensor_tensor(out=nb[:], in0=nb[:], in1=bt[:], op=ALU.add), [nk, "b"], [nk])
            self.dma("sp", xdv[si], nb[:], [nk], [("xd", si)])

        for si in range(NSUB + 2):
            if si < NSUB:
                stageA(si)
            if 0 <= si - 1 < NSUB:
                stageB(si - 1)
            if xTv is not None and 0 <= si - 2 < NSUB:
                emit_tr(si - 2)
        P.flush()

    def phase_xa(self, i, xT_src, prefetch=None):
        P, nc, A = self.P, self.nc, self.A
        A.reset()
        A.cap = self.WRES_BASE
        wq = A.alloc("wq", [128, NCH, D], BF16)
        wkv = A.alloc("wkv", [128, NCH, 2 * D], BF16)
        self.load_w(wkv, self.w["xa_w_kv"][i], NCH, "wkv")
        self.load_w(wq, self.w["xa_w_q"][i], NCH, "wq")
        if prefetch is not None:
            prefetch()
        kT = A.alloc("kT", [128, NCH, MEM], BF16)
        vv = A.alloc("vv", [128, 2, D], BF16)
        xt = [A.alloc("xt", [128, NCH, 512], BF16) for _ in range(2)]
        qT2 = [A.alloc("qT", [128, NCH, 512], BF16) for _ in range(2)]
        pb = [A.alloc("pb", [128, 4 * MEM], F32) for _ in range(2)]
        PT = A.alloc("PT", [128, 8, 512], BF16)
        oT = [A.alloc("oT", [128, NCH, 512], BF16) for _ in range(2)]
        sm = [A.alloc("sm", [128, 16], F32) for _ in range(2)]
        wkeys = [("wkv", c) for c in range(NCH)]
        for dc in range(NCH):
            b = dc % 2
            ps = self.pst[:, b, 0:MEM]
            for kc in range(NCH):
                P.add("pe", lambda ps=ps, kc=kc, dc=dc: nc.tensor.matmul(
                    ps, wkv[:, kc, dc * 128:(dc + 1) * 128], self.memT[:, kc, :], start=(kc == 0), stop=(kc == NCH - 1)),
                    wkeys + ["memT"], [("ps", b)])
            P.add("act", lambda ps=ps, dc=dc: nc.scalar.activation(out=kT[:, dc, :], in_=ps, func=AF.Identity),
                  [("ps", b)], ["kT"])
        for mc in range(2):
            for half in range(2):
                b = 2 + (mc * 2 + half) % 2
                ps = self.pst[:, b, :]
                for kc in range(NCH):
                    P.add("pe", lambda ps=ps, kc=kc, mc=mc, half=half: nc.tensor.matmul(
                        ps, self.memT[:, kc, mc * 128:(mc + 1) * 128], wkv[:, kc, D + half * 512:D + (half + 1) * 512],
                        start=(kc == 0), stop=(kc == NCH - 1)), wkeys + ["memT"], [("ps", b)])
                P.add("dve", lambda ps=ps, mc=mc, half=half: nc.vector.tensor_copy(
                    out=vv[:, mc, half * 512:(half + 1) * 512], in_=ps), [("ps", b)], ["vv"])
        xTv = xT_src.rearrange("(c p) t -> p c t", p=128)
        oTv = self.midT.rearrange("(c p) t -> p c t", p=128)
        scale = 256 ** -0.5
        def load_xt(tt):
            if tt < NTT:
                self.dma("sp", xt[tt % 2][:], xTv[:, :, tt * 512:(tt + 1) * 512], [], [("xt", tt % 2)])

        load_xt(0)

        def emit_q(tt, k):
            bi = tt % 2
            qT = qT2[bi]
            for dc in (2 * k, 2 * k + 1):
                b = dc % 2
                ps = self.pst[:, b, :]
                for kc in range(NCH):
                    P.add("pe", lambda ps=ps, kc=kc, dc=dc, bi=bi: nc.tensor.matmul(
                        ps, wq[:, kc, dc * 128:(dc + 1) * 128], xt[bi][:, kc, :], start=(kc == 0), stop=(kc == NCH - 1)),
                        [("wq", kc), ("xt", bi)], [("ps", b)])
                P.add("act", lambda ps=ps, dc=dc, qT=qT: nc.scalar.activation(out=qT[:, dc, :], in_=ps, func=AF.Identity, scale=scale),
                      [("ps", b)], [("qT", bi, dc)])

        def emit_rest(tt):
            bi = tt % 2
            qT = qT2[bi]
            load_xt(tt + 1)

            def qn(k):
                if tt + 1 < NTT:
                    emit_q(tt + 1, k)

            def scores(s4):
                sb_ = 2 + 2 * (s4 % 2)
                for hh in range(4):
                    ps = self.pst[:, sb_ + hh // 2, (hh % 2) * 256:(hh % 2 + 1) * 256]
                    for dl in range(2):
                        P.add("pe", lambda ps=ps, hh=hh, dl=dl, s4=s4: nc.tensor.matmul(
                            ps, qT[:, 2 * hh + dl, s4 * 128:(s4 + 1) * 128], kT[:, 2 * hh + dl, :],
                            start=(dl == 0), stop=(dl == 1)),
                            [("qT", bi, 2 * hh + dl), "kT"], [("ps", sb_ + hh // 2)])
                pss = self.pst[:, sb_:sb_ + 2, :]
                smt = sm[s4 % 2]
                pbt = pb[s4 % 2]
                sk = ("sm", s4 % 2)
                P.add("dve", lambda pss=pss, smt=smt: nc.vector.reduce_max(
                    out=smt[:, 0:4], in_=pss.rearrange("p a (h m) -> p (a h) m", h=2), axis=AX.X),
                    [("ps", sb_), ("ps", sb_ + 1)], [(sk, 0)])
                P.add("dve", lambda smt=smt: nc.vector.tensor_scalar(
                    out=smt[:, 4:8], in0=smt[:, 0:4], scalar1=-1.0, scalar2=None, op0=ALU.mult), [(sk, 0)], [(sk, 1)])
                for hh in range(4):
                    ps = self.pst[:, sb_ + hh // 2, (hh % 2) * 256:(hh % 2 + 1) * 256]
                    P.add("act", lambda ps=ps, hh=hh, smt=smt, pbt=pbt: nc.scalar.activation(
                        out=pbt[:, hh * 256:(hh + 1) * 256], in_=ps, func=AF.Exp, bias=smt[:, 4 + hh:5 + hh],
                        accum_out=smt[:, 8 + hh:9 + hh]),
                        [("ps", sb_ + hh // 2), (sk, 1)], [("pb", s4 % 2, hh), (sk, 2, hh)])
                P.add("dve", lambda smt=smt: nc.vector.reciprocal(out=smt[:, 12:16], in_=smt[:, 8:12]),
                      [(sk, 2, hh) for hh in range(4)], [(sk, 3)])
                for hh in range(4):
                    P.add("act", lambda hh=hh, smt=smt, pbt=pbt: nc.scalar.activation(
                        out=pbt[:, hh * 256:(hh + 1) * 256], in_=pbt[:, hh * 256:(hh + 1) * 256], func=AF.Identity,
                        scale=smt[:, 12 + hh:13 + hh]),
                        [("pb", s4 % 2, hh), (sk, 3)], [("pb", s4 % 2, hh)])

            def transposes(s4):
                pbt = pb[s4 % 2]
                for half in range(2):
                    b = 6 + half
                    ps = self.pst[:, b, :]
                    for j4 in range(4):
                        j = half * 4 + j4
                        P.add("pe", lambda ps=ps, j4=j4, j=j, pbt=pbt: nc.tensor.transpose(
                            ps[:, j4 * 128:(j4 + 1) * 128], pbt[:, j * 128:(j + 1) * 128], self.ident[:]),
                            [("pb", s4 % 2, j // 2), "ident"], [("ps", b)])
                    outap = PT[:, half * 4:(half + 1) * 4, s4 * 128:(s4 + 1) * 128]
                    inap = ps.rearrange("p (c t) -> p c t", c=4)
                    if half == 0:
                        P.add("act", lambda outap=outap, inap=inap: nc.scalar.activation(out=outap, in_=inap, func=AF.Identity),
                              [("ps", b)], [("PT", s4, half)])
                    else:
                        P.add("dve", lambda outap=outap, inap=inap: nc.vector.tensor_copy(out=outap, in_=inap),
                              [("ps", b)], [("PT", s4, half)])

            for s4 in range(4):
                scores(s4)
                if s4 > 0:
                    transposes(s4 - 1)
                qn(s4)
            transposes(3)
            ptk = [("PT", a, hh) for a in range(4) for hh in range(2)]
            for dc in range(NCH):
                hh = dc // 2
                b = dc % 2
                ps = self.pst[:, b, :]
                for mc in range(2):
                    P.add("pe", lambda ps=ps, mc=mc, dc=dc, hh=hh: nc.tensor.matmul(
                        ps, vv[:, mc, dc * 128:(dc + 1) * 128], PT[:, 2 * hh + mc, :], start=(mc == 0), stop=(mc == 1)),
                        ["vv"] + ptk, [("ps", b)])
                if dc % 2 == 0:
                    P.add("act", lambda ps=ps, dc=dc, bi=bi: nc.scalar.activation(out=oT[bi][:, dc, :], in_=ps, func=AF.Identity),
                          [("ps", b)], [("oT", bi, dc)])
                else:
                    P.add("dve", lambda ps=ps, dc=dc, bi=bi: nc.vector.tensor_copy(out=oT[bi][:, dc, :], in_=ps),
                          [("ps", b)], [("oT", bi, dc)])
            self.dma("sp", oTv[:, :, tt * 512:(tt + 1) * 512], oT[bi][:], [("oT", bi, dc) for dc in range(NCH)], [("oTd", tt)])

        for k in range(4):
            emit_q(0, k)
        for tt in range(NTT):
            emit_rest(tt)
        P.flush()

    def phase_ffn_a(self, i, xT_src, prefetch_chunk=None):
        P, nc, A = self.P, self.nc, self.A
        A.reset()
        A.cap = self.WRES_BASE
        NP = FF // 128
        xt = A.alloc("xt", [128, NCH, S], BF16)
        xTv = xT_src.rearrange("(c p) t -> p c t", p=128)
        for tt in range(NTT):
            self.dma("sp", xt[:, :, tt * 512:(tt + 1) * 512], xTv[:, :, tt * 512:(tt + 1) * 512], [], [("xt", tt)])
        cwraw = A.alloc("cwraw", [88, 2, 128], F32)
        cw2 = A.alloc("cw2", [128, 2, 88], F32)
        cwv = self.w["ffn_conv_w"][i]
        self.dma("sp", cwraw[:, 0, :], cwv[0:2].rearrange("k (c p) -> (k c) p", p=128), [], [("cwraw", 0)])
        self.dma("sp", cwraw[0:44, 1, :], cwv[2].rearrange("(c p) -> c p", p=128), [], [("cwraw", 1)])
        self.dma("sp", cwraw[44:88, 1, :], self.w["ffn_conv_b"][i].rearrange("(c p) -> c p", p=128), [], [("cwraw", 2)])
        for blk in range(2):
            P.add("pe", lambda blk=blk: nc.tensor.transpose(self.pst[:, 7, blk * 128:blk * 128 + 88], cwraw[:, blk, :], self.ident[0:88, 0:88]),
                  [("cwraw", 0), ("cwraw", 1), ("cwraw", 2), "ident"], [("ps", 7)])
        P.add("dve", lambda: nc.vector.tensor_copy(out=cw2[:], in_=self.pst[:, 7, 0:256].rearrange("p (a n) -> p a n", a=2)[:, :, 0:88]),
              [("ps", 7)], ["cw"])
        NPq = 2 * NP

        def cwk(k, ch):
            if k == 0:
                return cw2[:, 0, ch:ch + 1]
            if k == 1:
                return cw2[:, 0, NPq + ch:NPq + ch + 1]
            if k == 2:
                return cw2[:, 1, ch:ch + 1]
            return cw2[:, 1, NPq + ch:NPq + ch + 1]
        wu = [A.alloc("wu", [128, NCH, 256], BF16) for _ in range(2)]
        U = [A.alloc("U", [128, 2 + S], F32) for _ in range(2)]
        ac = [A.alloc("ac", [128, S], BF16) for _ in range(2)]
        for w_ in range(2):
            P.add("pool", lambda w_=w_: nc.gpsimd.memset(U[w_][:, 0:2], 0.0), [], [("U", w_, -1)])
        wv = self.w["ffn_w_up"][i].rearrange("(c p) n -> p c n", p=128)
        av = self.actT.rearrange("(c p) t -> c p t", p=128)
        cv = [[A.alloc("cv3", [128, 512], F32) for _ in range(3)] for _ in range(2)]
        steps = [(j, tt) for j in range(NP) for tt in range(NTT)]

        def load_pair(j):
            if j >= NP:
                return
            wb = j % 2
            for w_ in range(2):
                col = w_ * FF + j * 128
                self.dma("pool", wu[wb][:, :, w_ * 128:(w_ + 1) * 128], wv[:, :, col:col + 128], [], [("wu", wb, w_)])

        def stA(q):
            j, tt = steps[q]
            wb = j % 2
            r = q % 3
            if tt == 1:
                load_pair(j + 1)
            if tt == 4 and prefetch_chunk is not None:
                prefetch_chunk(j)
            for w_ in range(2):
                b = (q % 4) * 2 + w_
                ps = self.pst[:, b, :]
                ch = w_ * NP + j
                for kc in range(NCH):
                    P.add("pe", lambda ps=ps, kc=kc, wb=wb, w_=w_, tt=tt: nc.tensor.matmul(
                        ps, wu[wb][:, kc, w_ * 128:(w_ + 1) * 128], xt[:, kc, tt * 512:(tt + 1) * 512],
                        start=(kc == 0), stop=(kc == NCH - 1)),
                        [("wu", wb, w_), ("xt", tt)], [("ps", b)])
                c_ = cv[w_][r]
                ck = ("cv", w_, r)
                P.add("act", lambda ps=ps, w_=w_, tt=tt: nc.scalar.activation(
                    out=U[w_][:, 2 + tt * 512:2 + (tt + 1) * 512], in_=ps, func=AF.Identity),
                    [("ps", b)], [("U", w_, tt)])
                P.add("act", lambda ps=ps, c_=c_, ch=ch: nc.scalar.activation(
                    out=c_[:], in_=ps, func=AF.Identity, scale=cwk(2, ch), bias=cwk(3, ch)),
                    [("ps", b), "cw"], [ck])
                P.add("dve", lambda c_=c_, w_=w_, tt=tt, ch=ch: nc.vector.scalar_tensor_tensor(
                    out=c_[:], in0=U[w_][:, 1 + tt * 512:1 + (tt + 1) * 512], scalar=cwk(1, ch), in1=c_[:],
                    op0=ALU.mult, op1=ALU.add), [("U", w_, tt), ("U", w_, tt - 1), ck, "cw"], [ck])
                P.add("dve", lambda c_=c_, w_=w_, tt=tt, ch=ch: nc.vector.scalar_tensor_tensor(
                    out=c_[:], in0=U[w_][:, tt * 512:(tt + 1) * 512], scalar=cwk(0, ch), in1=c_[:],
                    op0=ALU.mult, op1=ALU.add), [("U", w_, tt), ("U", w_, tt - 1), ck, "cw"], [ck])

        def stB(q):
            j, tt = steps[q]
            wb = j % 2
            r = q % 3
            cg = cv[1][r]
            P.add("act", lambda cg=cg: nc.scalar.activation(out=cg[:], in_=cg[:], func=AF.Gelu_apprx_tanh),
                  [("cv", 1, r)], [("cv", 1, r)])
            P.add("pool", lambda r=r, wb=wb, tt=tt: nc.gpsimd.tensor_tensor(
                out=ac[wb][:, tt * 512:(tt + 1) * 512], in0=cv[0][r][:], in1=cv[1][r][:], op=ALU.mult),
                [("cv", 0, r), ("cv", 1, r)], [("ac", wb, tt)])
            if tt == NTT - 1:
                self.dma("sp", av[j], ac[wb][:], [("ac", wb, t2) for t2 in range(NTT)], [("acd", j)])

        load_pair(0)
        NQ = len(steps)
        for q in range(NQ + 1):
            if q < NQ:
                stA(q)
            if q >= 1:
                stB(q - 1)
        P.flush()

    def phase_sb(self, j, xT_src, prefetch=None):
        self.phase_sb_qkv(j, xT_src)
        self.phase_sb_attn(prefetch)

    def phase_sb_qkv(self, j, xT_src):
        P, nc, A = self.P, self.nc, self.A
        A.reset()
        A.cap = 229312
        wt = A.alloc("wqkv", [128, NCH, 3 * D], BF16)
        self.load_w(wt, self.w["sb_w_qkv"][j], NCH, "wqkv")
        xt = [A.alloc("xt", [128, NCH, 512], BF16) for _ in range(2)]
        qk = [A.alloc("qk", [128, 16, 512], BF16) for _ in range(2)]
        vt = [A.alloc("vt", [128, 4, D], BF16) for _ in range(2)]
        xTv = xT_src.rearrange("(c p) t -> p c t", p=128)
        qTv = self.qT.rearrange("(c p) t -> p c t", p=128)
        kTv = self.kT.rearrange("(c p) t -> p c t", p=128)
        vv = self.vtok.rearrange("(t s p) d -> t p s d", p=128, s=4)
        scale = 64 ** -0.5
        n = 0
        for tt in range(NTT):
            bi = tt % 2
            self.dma("sp", xt[bi][:], xTv[:, :, tt * 512:(tt + 1) * 512], [], [("xt", bi)])
            for oc in range(16):
                b = n % 4; n += 1
                ps = self.pst[:, b, :]
                for kc in range(NCH):
                    P.add("pe", lambda ps=ps, kc=kc, oc=oc, bi=bi: nc.tensor.matmul(
                        ps, wt[:, kc, oc * 128:(oc + 1) * 128], xt[bi][:, kc, :], start=(kc == 0), stop=(kc == NCH - 1)),
                        [("wqkv", kc), ("xt", bi)], [("ps", b)])
                if oc % 2 == 0:
                    P.add("act", lambda ps=ps, oc=oc, bi=bi: nc.scalar.activation(
                        out=qk[bi][:, oc, :], in_=ps, func=AF.Identity, scale=(scale if oc < 8 else 1.0)),
                        [("ps", b)], [("qk", bi, oc)])
                else:
                    P.add("dve", lambda ps=ps, oc=oc, bi=bi: nc.vector.tensor_scalar(
                        out=qk[bi][:, oc, :], in0=ps, scalar1=(scale if oc < 8 else 1.0), scalar2=None, op0=ALU.mult),
                        [("ps", b)], [("qk", bi, oc)])
            self.dma("sp", qTv[:, :, tt * 512:(tt + 1) * 512], qk[bi][:, 0:8, :], [("qk", bi, oc) for oc in range(8)], [("qTd", tt)])
            self.dma("sp", kTv[:, :, tt * 512:(tt + 1) * 512], qk[bi][:, 8:16, :], [("qk", bi, oc) for oc in range(8, 16)], [("kTd", tt)])
            for s4 in range(4):
                for half in range(2):
                    b = n % 4; n += 1
                    ps = self.pst[:, b, :]
                    for kc in range(NCH):
                        P.add("pe", lambda ps=ps, kc=kc, bi=bi, s4=s4, half=half: nc.tensor.matmul(
                            ps, xt[bi][:, kc, s4 * 128:(s4 + 1) * 128], wt[:, kc, 2 * D + half * 512:2 * D + (half + 1) * 512],
                            start=(kc == 0), stop=(kc == NCH - 1)), [("wqkv", kc), ("xt", bi)], [("ps", b)])
                    if half == 0:
                        P.add("act", lambda ps=ps, bi=bi, s4=s4, half=half: nc.scalar.activation(
                            out=vt[bi][:, s4, half * 512:(half + 1) * 512], in_=ps, func=AF.Identity),
                            [("ps", b)], [("vt", bi, s4, half)])
                    else:
                        P.add("dve", lambda ps=ps, bi=bi, s4=s4, half=half: nc.vector.tensor_copy(
                            out=vt[bi][:, s4, half * 512:(half + 1) * 512], in_=ps),
                            [("ps", b)], [("vt", bi, s4, half)])
            self.dma("sp", vv[tt], vt[bi][:], [("vt", bi, a, hh) for a in range(4) for hh in range(2)], [("vd", tt)])
        P.flush()

    def phase_sb_attn(self, prefetch=None):
        P, nc, A = self.P, self.nc, self.A
        A.reset()
        A.cap = self.WRES_BASE
        if prefetch is not None:
            prefetch()
        qTp = [A.alloc("qTp", [128, S], BF16) for _ in range(2)]
        kz = [[A.alloc("kz", [128, S], BF16) for _ in range(2)] for _ in range(2)]
        vp = [A.alloc("vp", [128, NSUB, 128], BF16) for _ in range(2)]
        oTp = [A.alloc("oTp", [128, S], BF16) for _ in range(2)]
        Ef = [A.alloc("Ef", [128, 1024], F32) for _ in range(2)]
        SPb = [A.alloc("SPb", [128, 1024], BF16) for _ in range(3)]
        Wb = [A.alloc("Wb", [128, 1024], BF16) for _ in range(3)]
        RS = A.alloc("RS", [128, 512], BF16)
        zer = A.alloc("zer", [128, 512], BF16)
        P.add("pool", lambda: nc.gpsimd.memset(zer[:], 0.0), [], ["zer"])
        for pb_ in range(2):
            for hl in range(2):
                lo = 64 * (1 - hl)
                P.add("pool", lambda pb_=pb_, hl=hl, lo=lo: nc.gpsimd.memset(kz[pb_][hl][lo:lo + 64, :], 0.0),
                      [], [("kzz", pb_, hl)])
        vv = self.vtok.rearrange("(n p) d -> p n d", p=128)
        negtri = self.tri[:, 0, :]
        negone = self.tri[:, 1, :]
        mask = self.tri[:, 2, :]
        units = []
        qcount = 0
        for c in range(NCH):
            for hl in range(2):
                for qi in range(NTT):
                    first = True
                    for r in range(3, -1, -1):
                        sc = 4 * qi + r
                        units.append(dict(c=c, hl=hl, qi=qi, ch=[sc], c0=128 * r, diag=True, first=first,
                                          last=(sc == 0), qn=qcount))
                        first = False
                    for sa in range(4 * qi - 1, 0, -2):
                        units.append(dict(c=c, hl=hl, qi=qi, ch=[sa, sa - 1], c0=0, diag=False, first=False,
                                          last=(sa - 1 == 0), qn=qcount))
                    qcount += 1
        loaded = set()

        def load_pair(c):
            if c in loaded or c >= NCH:
                return
            loaded.add(c)
            pb_ = c % 2
            self.dma("sp", qTp[pb_][:], self.qT[c * 128:(c + 1) * 128, :], [], [("qTp", pb_)])
            for hl in range(2):
                lo = 64 * hl
                self.dma("sp", kz[pb_][hl][lo:lo + 64, :], self.kT[c * 128 + lo:c * 128 + lo + 64, :], [], [("kz", pb_, hl)])
            for n0 in range(0, NSUB, 8):
                self.dma("sp", vp[pb_][:, n0:n0 + 8, :], vv[:, n0:n0 + 8, c * 128:(c + 1) * 128], [], [("vp", pb_, n0)])

        def zmm(ps, u, sc, start, stop, wkeys):
            c, hl, qi, c0 = u["c"], u["hl"], u["qi"], u["c0"]
            pb_ = c % 2
            P.add("pe", lambda: nc.tensor.matmul(ps, kz[pb_][hl][:, sc * 128:(sc + 1) * 128],
                                                 qTp[pb_][:, qi * 512 + c0:(qi + 1) * 512], start=start, stop=stop),
                  [("kz", pb_, hl), ("kzz", pb_, hl), ("qTp", pb_)], wkeys)

        def stA(n):
            u = units[n]
            load_pair(u["c"])
            c0 = u["c0"]
            nchk = len(u["ch"])
            ef = Ef[n % 2]
            sp_ = SPb[n % 3]
            if nchk == 1:
                b = n % 2
                pkeys = [("ps", b)]
                ps = self.pst[:, b, c0:512]
                zmm(ps, u, u["ch"][0], True, True, pkeys)
                efv, spv, psv = ef[:, c0:512], sp_[:, c0:512], ps
            else:
                pkeys = [("ps", 0), ("ps", 1)]
                for k_, sc in enumerate(u["ch"]):
                    zmm(self.pst[:, k_, :], u, sc, True, True, [("ps", k_)])
                efv, spv = ef[:, :], sp_[:, :]
                psv = self.pst[:, 0:2, :].rearrange("p a n -> p (a n)")
            P.add("act", lambda: nc.scalar.activation(out=efv, in_=psv, func=AF.Exp), pkeys, [("Ef", n % 2)])
            P.add("act", lambda: nc.scalar.activation(out=spv, in_=efv, func=AF.Ln, bias=1.0), [("Ef", n % 2)], [("SP", n % 3)])
            if u["diag"]:
                P.add("dve", lambda: nc.vector.tensor_tensor(out=sp_[:, c0:c0 + 128], in0=sp_[:, c0:c0 + 128], in1=mask, op=ALU.mult),
                      [("SP", n % 3), "tri"], [("SP", n % 3)])

        def stB(n):
            u = units[n]
            c0 = u["c0"]
            nchk = len(u["ch"])
            sp_ = SPb[n % 3]
            wb = Wb[n % 3]
            b0 = 2 + 2 * (n % 2)
            if u["first"]:
                P.add("dve", lambda: nc.vector.memset(RS[:], 0.0), [], ["RS"])
            if nchk == 1:
                pkeys = [("ps", b0)]
                ps = self.pst[:, b0, c0:512]
                zmm(ps, u, u["ch"][0], True, False, pkeys)
                P.add("pe", lambda: nc.tensor.matmul(ps, negtri, sp_[:, c0:512], start=False, stop=u["first"]),
                      [("SP", n % 3), "tri"], pkeys)
                if not u["first"]:
                    P.add("pe", lambda: nc.tensor.matmul(ps, negone, RS[:, c0:512], start=False, stop=True), ["RS", "tri"], pkeys)
                psv, wbv = ps, wb[:, c0:512]
            else:
                pkeys = [("ps", b0), ("ps", b0 + 1)]
                for k_, sc in enumerate(u["ch"]):
                    ps = self.pst[:, b0 + k_, :]
                    zmm(ps, u, sc, True, False, [("ps", b0 + k_)])
                    P.add("pe", lambda ps=ps, k_=k_: nc.tensor.matmul(ps, negtri, sp_[:, k_ * 512:(k_ + 1) * 512], start=False, stop=False),
                          [("SP", n % 3), "tri"], [("ps", b0 + k_)])
                    if k_ == 1:
                        P.add("pe", lambda ps=ps: nc.tensor.matmul(ps, negone, sp_[:, 0:512], start=False, stop=False),
                              [("SP", n % 3), "tri"], [("ps", b0 + k_)])
                    P.add("pe", lambda ps=ps: nc.tensor.matmul(ps, negone, RS[:, :], start=False, stop=True), ["RS", "tri"], [("ps", b0 + k_)])
                psv = self.pst[:, b0:b0 + 2, :].rearrange("p a n -> p (a n)")
                wbv = wb[:, :]
            P.add("act", lambda: nc.scalar.activation(out=wbv, in_=psv, func=AF.Exp), pkeys, [("Wb", n % 3)])
            if u["diag"]:
                P.add("dve", lambda: nc.vector.tensor_tensor(out=wb[:, c0:c0 + 128], in0=wb[:, c0:c0 + 128], in1=mask, op=ALU.mult),
                      [("Wb", n % 3), "tri"], [("Wb", n % 3)])
            if not u["last"]:
                if nchk == 1:
                    P.add("dve", lambda: nc.vector.tensor_tensor(out=RS[:, c0:512], in0=RS[:, c0:512], in1=sp_[:, c0:512], op=ALU.add),
                          ["RS", ("SP", n % 3)], ["RS"])
                else:
                    for k_ in range(2):
                        P.add("dve", lambda k_=k_: nc.vector.tensor_tensor(out=RS[:, :], in0=RS[:, :], in1=sp_[:, k_ * 512:(k_ + 1) * 512], op=ALU.add),
                              ["RS", ("SP", n % 3)], ["RS"])

        def stO(n):
            u = units[n]
            c, hl, qi, c0 = u["c"], u["hl"], u["qi"], u["c0"]
            pb_ = c % 2
            b = 6 + u["qn"] % 2
            wb = Wb[n % 3]
            if u["first"]:
                P.add("pe", lambda: nc.tensor.matmul(self.pst[:, b, :], vp[pb_][:, 0, :], zer[:], start=True, stop=False),
                      [("vp", pb_, 0), "zer"], [("ps", b)])
            nchk = len(u["ch"])
            for k_, sc in enumerate(u["ch"]):
                rhs = wb[:, c0:512] if nchk == 1 else wb[:, k_ * 512:(k_ + 1) * 512]
                P.add("pe", lambda sc=sc, rhs=rhs, k_=k_: nc.tensor.matmul(self.pst[:, b, c0:512], vp[pb_][:, sc, :], rhs, start=False,
                                                                  stop=(u["last"] and k_ == nchk - 1)),
                      [("vp", pb_, (sc // 8) * 8), ("Wb", n % 3)], [("ps", b)])
            if u["last"]:
                lo = 64 * hl
                P.add("dve", lambda: nc.vector.tensor_copy(out=oTp[pb_][lo:lo + 64, qi * 512:(qi + 1) * 512],
                                                           in_=self.pst[lo:lo + 64, b, :]),
                      [("ps", b)], [("oTp", pb_, hl, qi)])
                if hl == 1 and qi == NTT - 1:
                    self.dma("sp", self.midT[c * 128:(c + 1) * 128, :], oTp[pb_][:],
                             [("oTp", pb_, a_, q_) for a_ in range(2) for q_ in range(NTT)], [("oTd", c)])
                    load_pair(c + 2)

        load_pair(0)
        load_pair(1)
        N = len(units)
        for n in range(N + 2):
            if n < N:
                stA(n)
            if 0 <= n - 1 < N:
                stB(n - 1)
            if 0 <= n - 2 < N:
                stO(n - 2)
        P.flush()

    def phase_s5(self, j, xT_src):
        P, nc, A = self.P, self.nc, self.A
        A.reset()
        A.cap = 229312
        PI = float(np.pi)
        MUL, ADD, SUB = ALU.mult, ALU.add, ALU.subtract

        def TT(eng, out, a, b, op, rk, wk):
            h = P.h[eng]
            P.add(eng, lambda: h.tensor_tensor(out=out, in0=a, in1=b, op=op), rk, wk)

        def TS(eng, out, a, s1, s2, op0, op1, rk, wk):
            h = P.h[eng]
            if op1 is None:
                P.add(eng, lambda: h.tensor_scalar(out=out, in0=a, scalar1=s1, scalar2=None, op0=op0), rk, wk)
            else:
                P.add(eng, lambda: h.tensor_scalar(out=out, in0=a, scalar1=s1, scalar2=s2, op0=op0, op1=op1), rk, wk)

        def ACTF(out, a, func, rk, wk, **kw):
            P.add("act", lambda: nc.scalar.activation(out=out, in_=a, func=func, **kw), rk, wk)

        self._bank = 0

        def nb():
            b = self._bank
            self._bank = (b + 1) % 8
            return b

        cs5 = A.alloc("cs5", [128, 10, 128], F32)
        self.dma("sp", cs5[:], self.c_s5, [], ["cs5"])
        II = cs5[0:64, 0, :]
        bd = cs5[:, 1, :]
        aR = A.alloc("aR", [64, 64], F32); aI = A.alloc("aI", [64, 64], F32); ls = A.alloc("ls", [64, 64], F32)
        T1 = A.alloc("T1", [64, 64], F32); T2 = A.alloc("T2", [64, 64], F32)
        T3 = A.alloc("T3", [64, 64], F32); T4 = A.alloc("T4", [64, 64], F32)
        fr = A.alloc("fr", [64, 64], F32); fi = A.alloc("fi", [64, 64], F32)
        L = A.alloc("L", [64, 9, 2, 64], F32)
        AK = A.alloc("AK", [64, 10, 3, 64], F32)
        AKp = A.alloc("AKp", [128, 10, 3, 32], F32)
        Bt = A.alloc("Bt", [64, 2, 64, 16], F32)
        Bb = A.alloc("Bb", [64, 2, 64, 16], F32)
        U1 = A.alloc("U1", [64, 1024], F32); U2 = A.alloc("U2", [64, 1024], F32)
        Cnat = A.alloc("Cnat", [128, 2, 8, 64], F32)
        CT = A.alloc("CT", [64, 3, 1024], F32)
        dT = A.alloc("dT", [128, 8], F32)
        araw = A.alloc("araw", [64, 2, 64], F32)
        draw = A.alloc("draw", [8, 128], F32)
        self.dma("sp", araw[:, 0, :], self.w["s5_a_re"][j], [], [("araw", 0)])
        self.dma("sp", araw[:, 1, :], self.w["s5_a_im"][j], [], [("araw", 1)])
        self.dma("sp", draw[:], self.w["s5_d"][j].rearrange("(go q) -> go q", q=128), [], ["draw"])
        for ri, dst_ in enumerate((aR, aI)):
            P.add("pe", lambda ri=ri: nc.tensor.transpose(self.pst[0:64, 7, ri * 64:(ri + 1) * 64], araw[:, ri, :], self.ident[0:64, 0:64]),
                  [("araw", ri), "ident"], [("ps", 7)])
        P.add("pe", lambda: nc.tensor.transpose(self.pst[:, 7, 128:136], draw[:], self.ident[0:8, 0:8]), ["draw", "ident"], [("ps", 7)])
        P.add("dve", lambda: nc.vector.tensor_copy(out=aR[:], in_=self.pst[0:64, 7, 0:64]), [("ps", 7)], ["aR"])
        P.add("dve", lambda: nc.vector.tensor_copy(out=aI[:], in_=self.pst[0:64, 7, 64:128]), [("ps", 7)], ["aI"])
        P.add("dve", lambda: nc.vector.tensor_copy(out=dT[:], in_=self.pst[:, 7, 128:136]), [("ps", 7)], ["dT"])
        self.dma("sp", ls[:], self.w["s5_log_step"][j].partition_broadcast(64), [], ["ls"])
        for g0 in range(0, 64, 16):
            self.dma("sp", Bt[:, 0, g0:g0 + 16, :], self.w["s5_b_re"][j][g0:g0 + 16].rearrange("g p c -> p g c"), [], [("Bt0", g0)])
            self.dma("sp", Bt[:, 1, g0:g0 + 16, :], self.w["s5_b_im"][j][g0:g0 + 16].rearrange("g p c -> p g c"), [], [("Bt1", g0)])
        self.dma("sp", Cnat[:, 0], self.w["s5_c_re"][j].rearrange("(go gl) c p -> (gl c) go p", go=8), [], ["Cn0"])
        self.dma("sp", Cnat[:, 1], self.w["s5_c_im"][j].rearrange("(go gl) c p -> (gl c) go p", go=8), [], ["Cn1"])
        ACTF(ls[:], ls[:], AF.Exp, ["ls"], ["ls"])
        TT("dve", T1[:], aR[:], ls[:], MUL, ["aR", "ls"], ["T1"])
        ACTF(T1[:], T1[:], AF.Exp, ["T1"], ["T1"])
        TT("dve", T2[:], aI[:], ls[:], MUL, ["aI", "ls"], ["T2"])
        KI = A.alloc("KI", [64, 64], mybir.dt.int32)
        TS("dve", T3[:], T2[:], 1.0 / (2 * PI), None, MUL, None, ["T2"], ["T3"])
        P.add("dve", lambda: nc.vector.tensor_copy(out=KI[:], in_=T3[:]), ["T3"], ["KI"])
        P.add("dve", lambda: nc.vector.tensor_copy(out=T3[:], in_=KI[:]), ["KI"], ["T3"])
        P.add("dve", lambda: nc.vector.scalar_tensor_tensor(out=T2[:], in0=T3[:], scalar=-2 * PI, in1=T2[:], op0=MUL, op1=ADD),
              ["T3", "T2"], ["T2"])
        ACTF(T3[:], T2[:], AF.Sin, ["T2"], ["T3"], scale=0.5)
        ACTF(T4[:], T2[:], AF.Sin, ["T2"], ["T4"], scale=-0.5, bias=PI / 2)
        TT("dve", fr[:], T3[:], T4[:], MUL, ["T3", "T4"], ["fr"])
        TT("dve", T3[:], T3[:], T3[:], MUL, ["T3"], ["T3"])
        TT("dve", T4[:], T4[:], T4[:], MUL, ["T4"], ["T4"])
        TT("dve", T4[:], T4[:], T3[:], SUB, ["T4", "T3"], ["T4"])
        TS("dve", T3[:], fr[:], 2.0, None, MUL, None, ["fr"], ["T3"])
        lr = L[:, 1, 0, :]; li = L[:, 1, 1, :]
        TT("dve", lr, T1[:], T4[:], MUL, ["T1", "T4"], [("L", 1, 0)])
        TT("dve", li, T1[:], T3[:], MUL, ["T1", "T3"], [("L", 1, 1)])
        P.add("pool", lambda: nc.gpsimd.memset(L[:, 0, 0, :], 1.0), [], [("L", 0, 0)])
        P.add("pool", lambda: nc.gpsimd.memset(L[:, 0, 1, :], 0.0), [], [("L", 0, 1)])
        for k in range(1, 8):
            kr, ki = L[:, k, 0, :], L[:, k, 1, :]
            TT("dve", T1[:], kr, lr, MUL, [("L", k, 0), ("L", 1, 0)], ["T1"])
            TT("dve", T2[:], ki, li, MUL, [("L", k, 1), ("L", 1, 1)], ["T2"])
            TT("dve", L[:, k + 1, 0, :], T1[:], T2[:], SUB, ["T1", "T2"], [("L", k + 1, 0)])
            TT("dve", T3[:], kr, li, MUL, [("L", k, 0), ("L", 1, 1)], ["T3"])
            TT("dve", T4[:], ki, lr, MUL, [("L", k, 1), ("L", 1, 0)], ["T4"])
            TT("dve", L[:, k + 1, 1, :], T3[:], T4[:], ADD, ["T3", "T4"], [("L", k + 1, 1)])
        P.add("dve", lambda: nc.vector.tensor_copy(out=AK[:, 0, 0, :], in_=L[:, 8, 0, :]), [("L", 8, 0)], [("AK", 0)])
        P.add("dve", lambda: nc.vector.tensor_copy(out=AK[:, 0, 1, :], in_=L[:, 8, 1, :]), [("L", 8, 1)], [("AK", 0)])
        for l in range(9):
            ar_, ai_ = AK[:, l, 0, :], AK[:, l, 1, :]
            TT("dve", T1[:], ar_, ar_, MUL, [("AK", l)], ["T1"])
            TT("dve", T2[:], ai_, ai_, MUL, [("AK", l)], ["T2"])
            TT("dve", AK[:, l + 1, 0, :], T1[:], T2[:], SUB, ["T1", "T2"], [("AK", l + 1)])
            TT("dve", T3[:], ar_, ai_, MUL, [("AK", l)], ["T3"])
            TS("dve", AK[:, l + 1, 1, :], T3[:], 2.0, None, MUL, None, ["T3"], [("AK", l + 1)])
        akk = [("AK", l) for l in range(10)]
        TS("dve", AK[:, :, 2, :], AK[:, :, 1, :], -1.0, None, MUL, None, akk, ["AKn"])
        for l0 in range(0, 10, 2):
            b = nb()
            ps = self.pst[:, b, 0:384]
            P.add("pe", lambda ps=ps, l0=l0: nc.tensor.matmul(ps, II, AK[:, l0:l0 + 2, :, :].rearrange("p l c g -> p (l c g)"),
                                                              start=True, stop=True), akk + ["AKn", "cs5"], [("ps", b)])
            for gp in range(2):
                src_ = self.pst[gp * 64:(gp + 1) * 64, b, 0:384].rearrange("p (l c j two) -> p l c j two", l=2, c=3, two=2)[:, :, :, :, gp]
                P.add("dve", lambda src_=src_, gp=gp, l0=l0: nc.vector.tensor_copy(out=AKp[gp * 64:(gp + 1) * 64, l0:l0 + 2, :, :], in_=src_),
                      [("ps", b)], [("AKp", l0, gp)])
        akpk = [("AKp", l0, gp) for l0 in range(0, 10, 2) for gp in range(2)]
        TT("dve", T1[:], aR[:], aR[:], MUL, ["aR"], ["T1"])
        TT("dve", T2[:], aI[:], aI[:], MUL, ["aI"], ["T2"])
        TT("dve", T1[:], T1[:], T2[:], ADD, ["T1", "T2"], ["T1"])
        P.add("dve", lambda: nc.vector.reciprocal(out=T1[:], in_=T1[:]), ["T1"], ["T1"])
        TS("dve", T2[:], lr, -1.0, None, ADD, None, [("L", 1, 0)], ["T2"])
        TT("dve", T3[:], T2[:], aR[:], MUL, ["T2", "aR"], ["T3"])
        TT("dve", T4[:], li, aI[:], MUL, [("L", 1, 1), "aI"], ["T4"])
        TT("dve", T3[:], T3[:], T4[:], ADD, ["T3", "T4"], ["T3"])
        TT("dve", fr[:], T3[:], T1[:], MUL, ["T3", "T1"], ["fr"])
        TT("dve", T3[:], li, aR[:], MUL, [("L", 1, 1), "aR"], ["T3"])
        TT("dve", T4[:], T2[:], aI[:], MUL, ["T2", "aI"], ["T4"])
        TT("dve", T3[:], T3[:], T4[:], SUB, ["T3", "T4"], ["T3"])
        TT("dve", fi[:], T3[:], T1[:], MUL, ["T3", "T1"], ["fi"])
        frb = fr[:, :].unsqueeze(2).broadcast_to([64, 64, 16])
        fib = fi[:, :].unsqueeze(2).broadcast_to([64, 64, 16])
        U1v = U1[:, :].rearrange("p (g c) -> p g c", c=16)
        U2v = U2[:, :].rearrange("p (g c) -> p g c", c=16)
        TT("dve", U1v, Bt[:, 0], frb, MUL, [("Bt0", g0) for g0 in range(0, 64, 16)] + ["fr"], ["U1"])
        TT("dve", U2v, Bt[:, 1], fib, MUL, [("Bt1", g0) for g0 in range(0, 64, 16)] + ["fi"], ["U2"])
        TT("dve", Bb[:, 0], U1v, U2v, SUB, ["U1", "U2"], ["Bb0"])
        TT("dve", U1v, Bt[:, 1], frb, MUL, [("Bt1", g0) for g0 in range(0, 64, 16)] + ["fr"], ["U1"])
        TT("dve", U2v, Bt[:, 0], fib, MUL, [("Bt0", g0) for g0 in range(0, 64, 16)] + ["fi"], ["U2"])
        TT("dve", Bb[:, 1], U1v, U2v, ADD, ["U1", "U2"], ["Bb1"])
        for ri in range(2):
            for hb_ in range(2):
                b = nb()
                for g4 in range(4):
                    go = hb_ * 4 + g4
                    P.add("pe", lambda b=b, g4=g4, go=go, ri=ri: nc.tensor.transpose(
                        self.pst[0:64, b, g4 * 128:(g4 + 1) * 128], Cnat[:, ri, go, :], self.ident[:]),
                        [f"Cn{ri}", "ident"], [("ps", b)])
                P.add("dve", lambda b=b, ri=ri, hb_=hb_: nc.vector.tensor_copy(
                    out=CT[:, ri, hb_ * 512:(hb_ + 1) * 512], in_=self.pst[0:64, b, :]), [("ps", b)], [("CT", ri, hb_)])
        TS("dve", CT[:, 2, :], CT[:, 1, :], -1.0, None, MUL, None, [("CT", 1, 0), ("CT", 1, 1)], [("CT", 2)])
        ctk = [("CT", 0, 0), ("CT", 0, 1), ("CT", 1, 0), ("CT", 1, 1), ("CT", 2)]
        Lk = [("L", k, c) for k in range(9) for c in range(2)]
        X = A.alloc("X", [64, 2, 8, 128], F32)
        G = A.alloc("G", [64, 2, 8, 128], F32)
        Wd = A.alloc("Wd", [128, 8, 128], BF16)
        tmpW = A.alloc("tmpW", [128, 128], F32)
        WzT = A.alloc("WzT", [128, 4, 8, 2, 128], BF16)
        Gy = A.alloc("Gy", [128, 4, 8, 2, 128], BF16)
        wi = [A.alloc("wi", [128, NCH, 128], BF16) for _ in range(2)]
        xt = [A.alloc("xt", [128, NCH, 512], BF16) for _ in range(2)]
        uT = A.alloc("uT", [128, 8, 512], BF16)
        ZA = [[A.alloc("ZA", [128, 512], F32) for _ in range(2)] for _ in range(4)]
        SS = [[A.alloc("SS", [128, 512], F32) for _ in range(2)] for _ in range(2)]
        Hb = [[A.alloc("Hb", [128, 512], BF16) for _ in range(2)] for _ in range(4)]
        yT = A.alloc("yT", [128, S], BF16)
        for jl in range(4):
            for ri in range(2):
                P.add("pool", lambda jl=jl, ri=ri: nc.gpsimd.memset(Hb[jl][ri][:, 0:1], 0.0), [], [("Hb0", jl, ri)])
        xTv = xT_src.rearrange("(c p) t -> p c t", p=128)
        wv = self.w["s5_w_in"][j].rearrange("(c p) n -> p c n", p=128)
        V1 = U1[:, :].rearrange("p (a g c) -> p a g c", a=8, c=16)
        V2 = U2[:, :].rearrange("p (a g c) -> p a g c", a=8, c=16)
        nxt = 0
        for go in range(8):
            oc0 = go * 128
            gs = slice(go * 8, (go + 1) * 8)
            bsh = [64, 8, 8, 16]
            LRb = L[:, 0:8, 0, gs].unsqueeze(3).broadcast_to(bsh)
            LIb = L[:, 0:8, 1, gs].unsqueeze(3).broadcast_to(bsh)
            L1Rb = L[:, 1:9, 0, gs].unsqueeze(3).broadcast_to(bsh)
            L1Ib = L[:, 1:9, 1, gs].unsqueeze(3).broadcast_to(bsh)
            Bbr = Bb[:, 0, gs, :].unsqueeze(1).broadcast_to(bsh)
            Bbi = Bb[:, 1, gs, :].unsqueeze(1).broadcast_to(bsh)
            CTr = CT[:, 0, oc0:oc0 + 128].rearrange("p (g c) -> p g c", c=16).unsqueeze(1).broadcast_to(bsh)
            CTi = CT[:, 1, oc0:oc0 + 128].rearrange("p (g c) -> p g c", c=16).unsqueeze(1).broadcast_to(bsh)
            X0 = X[:, 0].rearrange("p a (g c) -> p a g c", c=16)
            X1 = X[:, 1].rearrange("p a (g c) -> p a g c", c=16)
            G0 = G[:, 0].rearrange("p a (g c) -> p a g c", c=16)
            G1 = G[:, 1].rearrange("p a (g c) -> p a g c", c=16)
            TT("dve", V1, LRb, Bbr, MUL, Lk + ["Bb0"], ["U1"])
            TT("dve", V2, LIb, Bbi, MUL, Lk + ["Bb1"], ["U2"])
            TT("dve", X0, V1, V2, SUB, ["U1", "U2"], ["X0"])
            TT("dve", V1, LRb, Bbi, MUL, Lk + ["Bb1"], ["U1"])
            TT("dve", V2, LIb, Bbr, MUL, Lk + ["Bb0"], ["U2"])
            TT("dve", X1, V1, V2, ADD, ["U1", "U2"], ["X1"])
            TT("dve", V1, L1Rb, CTr, MUL, Lk + ctk, ["U1"])
            TT("dve", V2, L1Ib, CTi, MUL, Lk + ctk, ["U2"])
            TT("dve", G0, V1, V2, SUB, ["U1", "U2"], ["G0"])
            TT("dve", V1, L1Ib, CTr, MUL, Lk + ctk, ["U1"])
            TT("dve", V2, L1Rb, CTi, MUL, Lk + ctk, ["U2"])
            TT("dve", V1, V1, V2, ADD, ["U1", "U2"], ["U1"])
            TS("dve", G1, V1, -1.0, None, MUL, None, ["U1"], ["G1"])
            for half in range(2):
                b = nb()
                for d4 in range(4):
                    dl = half * 4 + d4
                    ps = self.pst[:, b, d4 * 128:(d4 + 1) * 128]
                    P.add("pe", lambda ps=ps, dl=dl, oc0=oc0: nc.tensor.matmul(ps, X[:, 0, dl, :], CT[:, 0, oc0:oc0 + 128], start=True, stop=False),
                          ["X0"] + ctk, [("ps", b)])
                    P.add("pe", lambda ps=ps, dl=dl, oc0=oc0: nc.tensor.matmul(ps, X[:, 1, dl, :], CT[:, 2, oc0:oc0 + 128], start=False, stop=True),
                          ["X1"] + ctk, [("ps", b)])
                psv = self.pst[:, b, :].rearrange("p (a n) -> p a n", a=4)
                TT("dve", Wd[:, half * 4:(half + 1) * 4, :], psv, bd.unsqueeze(1).broadcast_to([128, 4, 128]), MUL,
                   [("ps", b), "cs5"], [("Wd", half)])
                if half == 0:
                    TT("dve", tmpW[:], self.pst[:, b, 0:128], bd, MUL, [("ps", b), "cs5"], ["tmpW"])
                    P.add("dve", lambda go=go: nc.vector.scalar_tensor_tensor(
                        out=Wd[:, 0, :], in0=self.ident[:], scalar=dT[:, go:go + 1], in1=tmpW[:], op0=MUL, op1=ADD),
                        ["tmpW", "ident", "dT", ("Wd", 0)], [("Wd", 0)])
            for bk in range(4):
                b = nb()
                for i4 in range(4):
                    idx = bk * 4 + i4
                    s_, ri = divmod(idx, 2)
                    ps = self.pst[:, b, i4 * 128:(i4 + 1) * 128]
                    P.add("pe", lambda ps=ps, s_=s_, ri=ri: nc.tensor.matmul(ps, X[:, ri, 7 - s_, :], II, start=True, stop=True),
                          ["X0", "X1", "cs5"], [("ps", b)])
                psv = self.pst[:, b, :].rearrange("p (a n) -> p a n", a=4)
                for jl in range(4):
                    TT("dve", WzT[:, jl, 2 * bk:2 * bk + 2, :, :].rearrange("p a b n -> p (a b) n"), psv,
                       cs5[:, 2 + jl, :].unsqueeze(1).broadcast_to([128, 4, 128]), MUL, [("ps", b), "cs5"], [("WzT", jl, bk)])
            for bk in range(4):
                b = nb()
                for i4 in range(4):
                    idx = bk * 4 + i4
                    t_, ri = divmod(idx, 2)
                    ps = self.pst[:, b, i4 * 128:(i4 + 1) * 128]
                    P.add("pe", lambda ps=ps, t_=t_, ri=ri: nc.tensor.matmul(ps, II, G[:, ri, t_, :], start=True, stop=True),
                          ["G0", "G1", "cs5"], [("ps", b)])
                psv = self.pst[:, b, :].rearrange("p (a n) -> p a n", a=4)
                for jl in range(4):
                    TT("dve", Gy[:, jl, 2 * bk:2 * bk + 2, :, :].rearrange("p a b n -> p (a b) n"), psv,
                       cs5[:, 6 + jl, :].unsqueeze(1).broadcast_to([128, 4, 128]), MUL, [("ps", b), "cs5"], [("Gy", jl, bk)])
            wb = go % 2
            self.dma("pool", wi[wb][:], wv[:, :, oc0:oc0 + 128], [], [("wi", wb)])
            for tt in range(NTT):
                xb = nxt % 2; nxt += 1
                self.dma("sp", xt[xb][:], xTv[:, :, tt * 512:(tt + 1) * 512], [], [("xt", xb)])
                b = nb()
                ps = self.pst[:, b, :]
                for kc in range(NCH):
                    P.add("pe", lambda ps=ps, kc=kc, wb=wb, xb=xb: nc.tensor.matmul(
                        ps, wi[wb][:, kc, :], xt[xb][:, kc, :], start=(kc == 0), stop=(kc == NCH - 1)),
                        [("wi", wb), ("xt", xb)], [("ps", b)])
                ACTF(uT[:, :, tt * 64:(tt + 1) * 64], ps.rearrange("p (b s) -> p s b", s=8), AF.Identity, [("ps", b)], [("uT", tt)])
            utk = [("uT", tt) for tt in range(NTT)]
            for jl in range(4):
                for ri in range(2):
                    b = nb()
                    ps = self.pst[:, b, :]
                    for s_ in range(8):
                        P.add("pe", lambda ps=ps, jl=jl, s_=s_, ri=ri: nc.tensor.matmul(
                            ps, WzT[:, jl, s_, ri, :], uT[:, s_, :], start=(s_ == 0), stop=(s_ == 7)),
                            utk + [("WzT", jl, bk) for bk in range(4)], [("ps", b)])
                    ACTF(ZA[jl][ri][:], ps, AF.Identity, [("ps", b)], [("ZA", jl, ri)])
            for jl in range(4):
                jg = go * 4 + jl
                si = jl % 2
                for l in range(9):
                    d = 1 << l
                    if l % 2 == 0:
                        src_, dst_, sk, dk = ZA[jl], SS[si], ("ZA", jl), ("SS", si)
                    else:
                        src_, dst_, sk, dk = SS[si], ZA[jl], ("SS", si), ("ZA", jl)
                    ar_ = AKp[:, l, 0, jg:jg + 1]; ai_ = AKp[:, l, 1, jg:jg + 1]; nai_ = AKp[:, l, 2, jg:jg + 1]
                    n_ = 512 - d

                    def stt(out, in0, sc, in1, rk, wk):
                        P.add("dve", lambda: nc.vector.scalar_tensor_tensor(out=out, in0=in0, scalar=sc, in1=in1, op0=MUL, op1=ADD), rk, wk)
                    stt(dst_[0][:, d:512], src_[0][:, 0:n_], ar_, src_[0][:, d:512], [sk + (0,)] + akpk, [dk + (0,)])
                    stt(dst_[0][:, d:512], src_[1][:, 0:n_], nai_, dst_[0][:, d:512], [sk + (1,), dk + (0,)] + akpk, [dk + (0,)])
                    stt(dst_[1][:, d:512], src_[1][:, 0:n_], ar_, src_[1][:, d:512], [sk + (1,)] + akpk, [dk + (1,)])
                    stt(dst_[1][:, d:512], src_[0][:, 0:n_], ai_, dst_[1][:, d:512], [sk + (0,), dk + (1,)] + akpk, [dk + (1,)])
                    for ri in range(2):
                        P.add("pool", lambda ri=ri, d=d, src_=src_, dst_=dst_: nc.gpsimd.tensor_copy(out=dst_[ri][:, 0:d], in_=src_[ri][:, 0:d]),
                              [sk + (ri,)], [dk + (ri,)])
                for ri in range(2):
                    ACTF(Hb[jl][ri][:, 1:512], SS[si][ri][:, 0:511], AF.Identity, [("SS", si, ri)], [("Hb", jl, ri)])
            yv = yT[:, :].rearrange("p (b t) -> p t b", t=8)
            for t_ in range(8):
                b = nb()
                ps = self.pst[:, b, :]
                mms = [(Wd[:, t_ - s_, :], uT[:, s_, :]) for s_ in range(t_ + 1)]
                mms += [(Gy[:, jl, t_, ri, :], Hb[jl][ri][:]) for jl in range(4) for ri in range(2)]
                rk = utk + [("Wd", 0), ("Wd", 1)] + [("Gy", jl, bk) for jl in range(4) for bk in range(4)] + \
                    [("Hb", jl, ri) for jl in range(4) for ri in range(2)] + [("Hb0", jl, ri) for jl in range(4) for ri in range(2)]
                for mi, (lt, rh) in enumerate(mms):
                    P.add("pe", lambda ps=ps, lt=lt, rh=rh, mi=mi, nm_=len(mms): nc.tensor.matmul(
                        ps, lt, rh, start=(mi == 0), stop=(mi == nm_ - 1)), rk, [("ps", b)])
                ACTF(yv[:, t_, :], ps, AF.Gelu_apprx_tanh, [("ps", b)], [("yT", t_)])
            self.dma("sp", self.midT[oc0:oc0 + 128, :], yT[:], [("yT", t_) for t_ in range(8)], [("yTd", go)])
        P.flush()

    def build(self):
        plan = self.plan
        cur = 0
        self.phase_prep(self.xT[cur])
        x_src = self.x_in
        n = len(plan)
        for idx, (kind, i) in enumerate(plan):
            last = idx == n - 1
            x_dst = self.y if last else self.xs
            xT_dst = None if last else self.xT[1 - cur]
            lg = self.w["ln_g"]
            lb = self.w["ln_b"]
            if kind == "xa":
                Wp = self.wres_tile(NCH, D)
                self.phase_xa(i, self.xT[cur], prefetch=lambda: self.load_w(Wp, self.w["xa_w_o"][i], NCH, "Wpre"))
                self.phase_proj_ln(self.midT, NCH, None, False, lg[i, 1], lb[i, 1], x_src, x_dst, xT_dst, Wpre=Wp)
            elif kind == "ffn":
                Wp = self.wres_tile(FF // 128, D)
                wdv = self.w["ffn_w_down"][i].rearrange("(c p) n -> p c n", p=128)
                self.phase_ffn_a(i, self.xT[cur], prefetch_chunk=lambda c, Wp=Wp, wdv=wdv: self.dma(
                    "pool", Wp[:, c, :], wdv[:, c, :], [], [("Wpre", c)]))
                self.phase_proj_ln(self.actT, FF // 128, None, False, lg[i, 2], lb[i, 2], x_src, x_dst, xT_dst, Wpre=Wp)
            elif kind == "s5":
                self.phase_s5(i // 2, self.xT[cur])
                self.phase_proj_ln(self.midT, NCH, self.w["s5_w_out"][i // 2], True, lg[i, 0], lb[i, 0], x_src, x_dst, xT_dst)
            elif kind == "sb":
                Wp = self.wres_tile(NCH, D)
                self.phase_sb(i // 2, self.xT[cur], prefetch=lambda: self.load_w(Wp, self.w["sb_w_o"][i // 2], NCH, "Wpre"))
                self.phase_proj_ln(self.midT, NCH, None, False, lg[i, 0], lb[i, 0], x_src, x_dst, xT_dst, Wpre=Wp)
            x_src = self.xs
            cur = 1 - cur
        self.P.finish()
        return self.nc


def make_consts():
    ident = np.eye(128, dtype=np.float32)
    tri = np.zeros((128, 4, 128), np.float32)
    j = np.arange(128)[:, None]
    s = np.arange(128)[None, :]
    tri[:, 0, :] = -(j >= s).astype(np.float32)
    tri[:, 1, :] = -1.0
    tri[:, 2, :] = (j < s).astype(np.float32)
    cs5 = np.zeros((128, 10, 128), np.float32)
    q = np.arange(128)
    cs5[:64, 0, :64] = np.eye(64); cs5[:64, 0, 64:] = np.eye(64)
    cs5[:, 1, :] = (q[:, None] // 16 == q[None, :] // 16)
    for jl in range(4):
        cs5[:, 2 + jl, :] = (q[:, None] // 16 == 2 * jl + q[None, :] // 64)
        cs5[:, 6 + jl, :] = (q[None, :] // 16 == 2 * jl + q[:, None] // 64)
    return ident, tri, cs5


FULL_PLAN = []
for _i in range(DEPTH):
    FULL_PLAN += [("s5" if _i % 2 == 0 else "sb", _i), ("xa", _i), ("ffn", _i)]

_CACHE = {}


def run_plan(plan, inputs, n_cores=8):
    key = tuple(plan)
    if key not in _CACHE:
        bld = Builder(list(plan))
        _CACHE[key] = (bld.build(), list(bld.w.keys()))
    nc, used = _CACHE[key]
    ident, tri, cs5 = make_consts()
    x = np.ascontiguousarray(inputs["x"])
    mem = np.ascontiguousarray(inputs["mem"])
    in_maps = []
    for c in range(n_cores):
        m = {k: np.ascontiguousarray(inputs[k]) for k in used}
        m["x"] = x[c]
        m["mem"] = mem[c]
        m["c_ident"] = ident
        m["c_tri"] = tri
        m["c_s5"] = cs5
        in_maps.append(m)
    res = run_bass_kernel_spmd(nc, in_maps, core_ids=list(range(n_cores)))
    return np.stack([np.asarray(r["y"]) for r in res.results], axis=0)


def kernel(**inputs):
    out = run_plan(FULL_PLAN, inputs, 8)
    return out.astype(np.float32)
```

```python
import numpy as np
import ml_dtypes
import concourse.bass as bass
import concourse.mybir as mybir
from concourse.bass_utils import run_bass_kernel_spmd

F32 = mybir.dt.float32
BF16 = mybir.dt.bfloat16
AF = mybir.ActivationFunctionType
ALU = mybir.AluOpType
AX = mybir.AxisListType

D = 1024
S = 4096
DEPTH = 4
MEM = 256
FF = 2816
NCH = D // 128
NSUB = S // 128
NTT = S // 512
LN_EPS = 1e-5
DN_ALPHA = (2.0 * DEPTH) ** 0.25


class Op:
    __slots__ = ("eng", "fn", "deps", "need_inc", "cnt", "is_dma", "dsem", "dval", "prev_dma")


class Prog:
    CENG = ("pe", "act", "dve", "pool")
    NDS = 8

    def __init__(self, nc):
        self.nc = nc
        self.h = {"pe": nc.tensor, "act": nc.scalar, "dve": nc.vector, "pool": nc.gpsimd, "sp": nc.sync}
        self.esem = {e: nc.semaphore("sem_" + e).__enter__() for e in self.CENG}
        self.dsem = {q: [nc.semaphore(f"dsem_{q}{i}").__enter__() for i in range(self.NDS)]
                     for q in ("sp", "act", "pool")}
        self.ecnt = {e: 0 for e in self.CENG}
        self.dcount = {q: 0 for q in self.dsem}
        self.dhist = {q: [] for q in self.dsem}
        self.dval = {q: [0] * self.NDS for q in self.dsem}
        self.known = {e: {} for e in self.h}
        self.ops = []
        self.lastw = {}
        self.readers = {}
        self.n_instr = 0

    def add(self, eng, fn, reads=(), writes=(), dma=False):
        op = Op()
        op.eng = eng; op.fn = fn; op.need_inc = False; op.cnt = None; op.is_dma = dma
        op.dsem = None; op.dval = None; op.prev_dma = None
        deps = {}
        for k in reads:
            w = self.lastw.get(k)
            if w is not None:
                deps[id(w)] = (w, "raw")
        for k in writes:
            w = self.lastw.get(k)
            if w is not None and id(w) not in deps:
                deps[id(w)] = (w, "waw")
            r = self.readers.get(k)
            if r:
                for o in r["eng"].values():
                    if id(o) not in deps:
                        deps[id(o)] = (o, "war")
                for o in r["dma"]:
                    if id(o) not in deps:
                        deps[id(o)] = (o, "war")
        for k in writes:
            self.lastw[k] = op
            self.readers[k] = {"eng": {}, "dma": []}
        for k in reads:
            r = self.readers.setdefault(k, {"eng": {}, "dma": []})
            if dma:
                r["dma"].append(op)
            else:
                r["eng"][eng] = op
        flt = []
        for (p, kind) in deps.values():
            if p is op:
                continue
            if not p.is_dma and not dma and p.eng == eng:
                if eng == "pe" or kind != "raw":
                    continue
            flt.append(p)
            if not p.is_dma:
                p.need_inc = True
        op.deps = flt
        if dma:
            q = eng
            k = self.dcount[q]; self.dcount[q] += 1
            slot = k % self.NDS
            op.dsem = self.dsem[q][slot]
            self.dval[q][slot] += 16
            op.dval = self.dval[q][slot]
        self.ops.append(op)
        return op

    def _wait(self, eng, sem, val):
        kn = self.known[eng]
        key = id(sem)
        if kn.get(key, 0) >= val:
            return
        kn[key] = val
        self.h[eng].wait_ge(sem, val)
        self.n_instr += 1

    def flush(self, barrier=True):
        last = {}
        for op in self.ops:
            if not op.is_dma:
                last[op.eng] = op
        if barrier:
            for op in last.values():
                op.need_inc = True
        for op in self.ops:
            if not op.is_dma and op.need_inc:
                self.ecnt[op.eng] += 1
                op.cnt = self.ecnt[op.eng]
        for op in self.ops:
            eng = op.eng
            if op.is_dma:
                if op.dval > 16:
                    self._wait(eng, op.dsem, op.dval - 16)
            for p in op.deps:
                if p.is_dma:
                    self._wait(eng, p.dsem, p.dval)
                else:
                    self._wait(eng, self.esem[p.eng], p.cnt)
            ins = op.fn()
            self.n_instr += 1
            if op.is_dma:
                ins.then_inc(op.dsem, 16)
            elif op.need_inc:
                ins.then_inc(self.esem[eng], 1)
        if barrier:
            for e in self.h:
                for e2 in self.CENG:
                    if e2 != e and self.ecnt[e2] > 0:
                        self._wait(e, self.esem[e2], self.ecnt[e2])
                for q in self.dsem:
                    for i in range(self.NDS):
                        if self.dval[q][i] > 0:
                            self._wait(e, self.dsem[q][i], self.dval[q][i])
        self.ops = []
        self.lastw = {}
        self.readers = {}

    def finish(self):
        self.flush(barrier=True)


class Arena:
    def __init__(self, nc, base=0, cap=204800):
        self.nc = nc; self.base = base; self.off = base; self.cap = cap; self.n = 0

    def reset(self):
        self.off = self.base

    def alloc(self, name, shape, dtype):
        esz = 4 if dtype == F32 else 2
        n = 1
        for s in shape[1:]:
            n *= s
        size = (n * esz + 63) // 64 * 64
        assert self.off + size <= self.cap, f"SBUF arena overflow at {name}: {self.off + size}"
        self.n += 1
        t = self.nc.alloc_sbuf_tensor_at(f"{name}_{self.n}", list(shape), dtype, offset=self.off)
        self.off += size
        return t


class Builder:
    def __init__(self, plan):
        self.plan = plan
        nc = bass.Bass("TRN2", target_bir_lowering=False)
        self.nc = nc
        self.P = Prog(nc)
        dt = nc.dram_tensor

        def ext(name, shape):
            return dt(name, list(shape), F32, kind="ExternalInput").ap()

        self.x_in = ext("x", [S, D])
        self.mem = ext("mem", [MEM, D])
        self._wshapes = dict([
            ("s5_w_in", [2, D, D]), ("s5_a_re", [2, 64, 64]), ("s5_a_im", [2, 64, 64]), ("s5_log_step", [2, 64]),
            ("s5_b_re", [2, 64, 64, 16]), ("s5_b_im", [2, 64, 64, 16]), ("s5_c_re", [2, 64, 16, 64]),
            ("s5_c_im", [2, 64, 16, 64]), ("s5_d", [2, D]), ("s5_w_out", [2, D, 2 * D]),
            ("sb_w_qkv", [2, D, 3 * D]), ("sb_w_o", [2, D, D]),
            ("xa_w_q", [4, D, D]), ("xa_w_kv", [4, D, 2 * D]), ("xa_w_o", [4, D, D]),
            ("ffn_w_up", [4, D, 2 * FF]), ("ffn_conv_w", [4, 3, 2 * FF]), ("ffn_conv_b", [4, 2 * FF]),
            ("ffn_w_down", [4, FF, D]), ("ln_g", [4, 3, D]), ("ln_b", [4, 3, D]),
        ])
        self._ext = ext

        class _W(dict):
            def __missing__(d, name):
                d[name] = self._ext(name, self._wshapes[name])
                return d[name]
        self.w = _W()
        self.c_ident = ext("c_ident", [128, 128])
        self.c_s5 = ext("c_s5", [128, 10, 128])
        self.c_tri = ext("c_tri", [128, 4, 128])
        self.y = dt("y", [S, D], F32, kind="ExternalOutput").ap()
        self.xs = dt("xs", [S, D], F32).ap()
        self.xT = [dt("xTa", [D, S], BF16).ap(), dt("xTb", [D, S], BF16).ap()]
        self.actT = dt("actT", [FF, S], BF16).ap()
        self.midT = dt("midT", [D, S], BF16).ap()
        self.qT = dt("qT", [D, S], BF16).ap()
        self.kT = dt("kT", [D, S], BF16).ap()
        self.vtok = dt("vtok", [S, D], BF16).ap()
        self.pst = nc.alloc_psum_tensor("pst", [128, 8, 512], F32)
        self.parena = Arena(nc, base=16640, cap=24832)
        self.ident = self.parena.alloc("ident", [128, 128], F32)
        self.memT = self.parena.alloc("memT", [128, NCH, MEM], BF16)
        self.tri = self.parena.alloc("tri", [128, 4, 128], BF16)
        self.A = Arena(nc, base=24832, cap=229312)

    WRES_BASE = 184256

    def wres_tile(self, kc, nout):
        self._wres_n = getattr(self, "_wres_n", 0) + 1
        return self.nc.alloc_sbuf_tensor_at(f"Wres_{self._wres_n}", [128, kc, nout], BF16, offset=self.WRES_BASE)

    def dma(self, q, out, in_, reads, writes, **kw):
        h = self.P.h[q]
        return self.P.add(q, lambda: h.dma_start(out=out, in_=in_, **kw), reads, writes, dma=True)

    def load_w(self, dst, src2d, kc_n, name, col0=0, ncol=None):
        v = src2d.rearrange("(c p) n -> p c n", p=128)
        ncol = ncol if ncol is not None else dst.shape[2]
        for c in range(kc_n):
            self.dma("pool", dst[:, c, 0:ncol], v[:, c, col0:col0 + ncol], [], [(name, c)])

    def bank(self, b, n=1):
        if n == 1:
            return self.pst[:, b, :]
        return self.pst[:, b:b + n, :]

    def phase_prep(self, xT_dst):
        P, nc, A = self.P, self.nc, self.A
        A.reset()
        A.cap = 229312
        self.dma("sp", self.ident[:], self.c_ident, [], ["ident"])
        self.dma("pool", self.tri[:], self.c_tri, [], ["tri"])
        memf = A.alloc("memf", [128, 2, D], F32)
        self.dma("sp", memf[:], self.mem.rearrange("(s p) d -> p s d", p=128), [], ["memf"])
        for mc in range(2):
            for half in range(2):
                b = (mc * 2 + half) % 2
                ps = self.pst[:, b, :]
                for c4 in range(4):
                    c = half * 4 + c4
                    P.add("pe", lambda ps=ps, c4=c4, c=c, mc=mc: nc.tensor.transpose(
                        ps[:, c4 * 128:(c4 + 1) * 128], memf[:, mc, c * 128:(c + 1) * 128], self.ident[:]),
                        ["memf", "ident"], [("ps", b)])
                P.add("act", lambda ps=ps, half=half, mc=mc: nc.scalar.activation(
                    out=self.memT[:, half * 4:(half + 1) * 4, mc * 128:(mc + 1) * 128],
                    in_=ps.rearrange("p (c t) -> p c t", c=4), func=AF.Identity),
                    [("ps", b)], ["memT"])
        xin = [A.alloc("xin", [128, 4, D], F32) for _ in range(2)]
        xTn = [A.alloc("xTn", [128, NCH, 512], BF16) for _ in range(2)]
        xv = self.x_in.rearrange("(t s p) d -> t p s d", p=128, s=4)
        xTv = xT_dst.rearrange("(c p) t -> p c t", p=128)
        for tt in range(NTT):
            bi = tt % 2
            self.dma("sp", xin[bi][:], xv[tt], [], [("xin", bi)])
            for s4 in range(4):
                si = tt * 4 + s4
                for half in range(2):
                    b = 2 + (si * 2 + half) % 4
                    ps = self.pst[:, b, :]
                    for c4 in range(4):
                        c = half * 4 + c4
                        P.add("pe", lambda ps=ps, c4=c4, c=c, bi=bi, s4=s4: nc.tensor.transpose(
                            ps[:, c4 * 128:(c4 + 1) * 128], xin[bi][:, s4, c * 128:(c + 1) * 128], self.ident[:]),
                            [("xin", bi), "ident"], [("ps", b)])
                    eng = "act" if half == 0 else "dve"
                    outap = xTn[bi][:, half * 4:(half + 1) * 4, s4 * 128:(s4 + 1) * 128]
                    inap = ps.rearrange("p (c t) -> p c t", c=4)
                    if eng == "act":
                        P.add("act", lambda outap=outap, inap=inap: nc.scalar.activation(out=outap, in_=inap, func=AF.Identity),
                              [("ps", b)], [("xTn", bi, s4, half)])
                    else:
                        P.add("dve", lambda outap=outap, inap=inap: nc.vector.tensor_copy(out=outap, in_=inap),
                              [("ps", b)], [("xTn", bi, s4, half)])
            self.dma("sp", xTv[:, :, tt * 512:(tt + 1) * 512], xTn[bi][:],
                     [("xTn", bi, s4, half) for s4 in range(4) for half in range(2)], [("xTd", tt)])
        P.flush()

    def phase_proj_ln(self, srcT, KC, W2d, glu, g_ap, b_ap, x_src, x_dst, xT_dst, Wpre=None):
        P, nc, A = self.P, self.nc, self.A
        A.reset()
        NOUT = 2 * D if glu else D
        if Wpre is not None:
            Wt = Wpre
            A.cap = self.WRES_BASE
        else:
            A.cap = 229312
            Wt = A.alloc("W", [128, KC, NOUT], BF16)
            self.load_w(Wt, W2d, KC, "W")
        gt = A.alloc("g", [128, D], F32)
        bt = A.alloc("b", [128, D], F32)
        self.dma("sp", gt[:], g_ap.partition_broadcast(128), [], ["g"])
        self.dma("sp", bt[:], b_ap.partition_broadcast(128), [], ["b"])
        src = [A.alloc("src", [128, KC, 512], BF16) for _ in range(2)]
        xin = [A.alloc("xin", [128, 4, D], F32) for _ in range(2)]
        tbuf = [A.alloc("tbuf", [128, D], F32) for _ in range(3)]
        nbuf = [A.alloc("nbuf", [128, D], F32) for _ in range(4)]
        xTn = [A.alloc("xTn", [128, NCH, 512], BF16) for _ in range(2)]
        st = [A.alloc("st", [128, 16], F32) for _ in range(3)]
        nm = [A.alloc("nm", [128, 2], F32) for _ in range(3)]
        if glu:
            sg = [A.alloc("sg", [128, 512], F32) for _ in range(2)]
            hb = [A.alloc("hb", [128, D], F32) for _ in range(2)]
        srcv = srcT.rearrange("(c p) t -> p c t", p=128)
        xsv = x_src.rearrange("(t s p) d -> t p s d", p=128, s=4)
        xdv = x_dst.rearrange("(n p) d -> n p d", p=128)
        xTv = xT_dst.rearrange("(c p) t -> p c t", p=128) if xT_dst is not None else None
        nhb = 1 if glu else 2
        nbk = 4 if glu else 2

        def emit_tr(si):
            tt, s4 = divmod(si, 4)
            bi = tt % 2
            nb = nbuf[si % 4]
            for half in range(2):
                b = 6 + half
                ps = self.pst[:, b, :]
                for c4 in range(4):
                    c = half * 4 + c4
                    P.add("pe", lambda ps=ps, c4=c4, c=c, nb=nb: nc.tensor.transpose(
                        ps[:, c4 * 128:(c4 + 1) * 128], nb[:, c * 128:(c + 1) * 128], self.ident[:]),
                        [("nbuf", si % 4), "ident"], [("ps", b)])
                outap = xTn[bi][:, half * 4:(half + 1) * 4, s4 * 128:(s4 + 1) * 128]
                inap = ps.rearrange("p (c t) -> p c t", c=4)
                P.add("act", lambda outap=outap, inap=inap: nc.scalar.activation(out=outap, in_=inap, func=AF.Identity),
                      [("ps", b)], [("xTn", bi, s4, half)])
            if s4 == 3:
                self.dma("sp", xTv[:, :, tt * 512:(tt + 1) * 512], xTn[bi][:],
                         [("xTn", bi, a, hh) for a in range(4) for hh in range(2)], [("xTd", tt)])

        def loads(tt):
            if tt >= NTT:
                return
            bi = tt % 2
            self.dma("sp", src[bi][:], srcv[:, :, tt * 512:(tt + 1) * 512], [], [("src", bi)])
            self.dma("sp", xin[bi][:], xsv[tt], [], [("xin", bi)])

        loads(0)

        def stageA(si):
            tt, s4 = divmod(si, 4)
            bi = tt % 2
            if s4 == 1:
                loads(tt + 1)
            if glu:
                hbt = hb[si % 2]
                for hc in range(2):
                    set_ = (2 * si + hc) % 3
                    bv, bg = 2 * set_, 2 * set_ + 1
                    for (bb_, c0_) in ((bv, hc * 512), (bg, D + hc * 512)):
                        ps = self.pst[:, bb_, :]
                        for kc in range(KC):
                            P.add("pe", lambda ps=ps, kc=kc, c0_=c0_, bi=bi, s4=s4: nc.tensor.matmul(
                                ps, src[bi][:, kc, s4 * 128:(s4 + 1) * 128], Wt[:, kc, c0_:c0_ + 512],
                                start=(kc == 0), stop=(kc == KC - 1)),
                                [("src", bi), ("W", kc)], [("ps", bb_)])
                    sgt = sg[(2 * si + hc) % 2]
                    sgk = ("sg", (2 * si + hc) % 2)
                    P.add("act", lambda sgt=sgt, bg=bg: nc.scalar.activation(out=sgt[:], in_=self.pst[:, bg, :], func=AF.Exp, scale=-1.0),
                          [("ps", bg)], [sgk])
                    P.add("dve", lambda sgt=sgt: nc.vector.tensor_scalar(out=sgt[:], in0=sgt[:], scalar1=1.0, scalar2=None, op0=ALU.add),
                          [sgk], [sgk])
                    P.add("dve", lambda sgt=sgt: nc.vector.reciprocal(out=sgt[:], in_=sgt[:]), [sgk], [sgk])
                    P.add("dve", lambda sgt=sgt, bv=bv, hbt=hbt, hc=hc: nc.vector.tensor_tensor(
                        out=hbt[:, hc * 512:(hc + 1) * 512], in0=self.pst[:, bv, :], in1=sgt[:], op=ALU.mult),
                        [("ps", bv), sgk], [("hb", si % 2, hc)])
                hsrc = hbt[:]
                hkeys = [("hb", si % 2, 0), ("hb", si % 2, 1)]
            else:
                hbi = si % 2
                b0 = hbi * 2
                for half in range(2):
                    ps = self.pst[:, b0 + half, :]
                    for kc in range(KC):
                        P.add("pe", lambda ps=ps, kc=kc, half=half, bi=bi, s4=s4: nc.tensor.matmul(
                            ps, src[bi][:, kc, s4 * 128:(s4 + 1) * 128], Wt[:, kc, half * 512:(half + 1) * 512],
                            start=(kc == 0), stop=(kc == KC - 1)),
                            [("src", bi), ("W", kc)], [("ps", b0 + half)])
                hsrc = self.pst[:, b0:b0 + 2, :].rearrange("p a n -> p (a n)")
                hkeys = [("ps", b0), ("ps", b0 + 1)]
            tb = tbuf[si % 3]
            stt = st[si % 3]
            tk = ("tbuf", si % 3)
            sk = ("st", si % 3)
            P.add("dve", lambda tb=tb, bi=bi, s4=s4, hsrc=hsrc: nc.vector.scalar_tensor_tensor(
                out=tb[:], in0=xin[bi][:, s4, :], scalar=DN_ALPHA, in1=hsrc, op0=ALU.mult, op1=ALU.add),
                [("xin", bi)] + hkeys, [tk])
            P.add("dve", lambda tb=tb, stt=stt: nc.vector.bn_stats(out=stt[:, 0:6], in_=tb[:, 0:512]), [tk], [(sk, 0)])
            P.add("dve", lambda tb=tb, stt=stt: nc.vector.bn_stats(out=stt[:, 6:12], in_=tb[:, 512:1024]), [tk], [(sk, 1)])
            P.add("dve", lambda stt=stt: nc.vector.bn_aggr(out=stt[:, 12:14], in_=stt[:, 0:12].rearrange("p (a b) -> p a b", a=2)),
                  [(sk, 0), (sk, 1)], [(sk, 2)])
            P.add("act", lambda stt=stt: nc.scalar.activation(out=stt[:, 14:15], in_=stt[:, 13:14], func=AF.Ln, bias=LN_EPS),
                  [(sk, 2)], [(sk, 3)])
            P.add("act", lambda stt=stt: nc.scalar.activation(out=stt[:, 15:16], in_=stt[:, 14:15], func=AF.Exp, scale=-0.5),
                  [(sk, 3)], [(sk, 4)])

        def stageB(si):
            tb = tbuf[si % 3]
            stt = st[si % 3]
            nmm = nm[si % 3]
            nb = nbuf[si % 4]
            tk = ("tbuf", si % 3)
            sk = ("st", si % 3)
            nk = ("nbuf", si % 4)
            P.add("dve", lambda stt=stt, nmm=nmm: nc.vector.scalar_tensor_tensor(
                out=nmm[:, 0:1], in0=stt[:, 12:13], scalar=-1.0, in1=stt[:, 15:16], op0=ALU.mult, op1=ALU.mult),
                [(sk, 2), (sk, 4)], [("nm", si % 3)])
            P.add("act", lambda nb=nb, tb=tb, stt=stt, nmm=nmm: nc.scalar.activation(
                out=nb[:], in_=tb[:], func=AF.Identity, scale=stt[:, 15:16], bias=nmm[:, 0:1]),
                [tk, (sk, 4), ("nm", si % 3)], [nk])
            P.add("dve", lambda nb=nb: nc.vector.tensor_tensor(out=nb[:], in0=nb[:], in1=gt[:], op=ALU.mult), [nk, "g"], [nk])
            P.add("pool", lambda nb=nb: nc.gpsimd.tensor_tensor(out=nb[:], in0=nb[:], in1=bt[:], op=ALU.add), [nk, "b"], [nk])
            self.dma("sp", xdv[si], nb[:], [nk], [("xd", si)])

        for si in range(NSUB + 2):
            if si < NSUB:
                stageA(si)
            if 0 <= si - 1 < NSUB:
                stageB(si - 1)
            if xTv is not None and 0 <= si - 2 < NSUB:
                emit_tr(si - 2)
        P.flush()

    def phase_xa(self, i, xT_src, prefetch=None):
        P, nc, A = self.P, self.nc, self.A
        A.reset()
        A.cap = self.WRES_BASE
        wq = A.alloc("wq", [128, NCH, D], BF16)
        wkv = A.alloc("wkv", [128, NCH, 2 * D], BF16)
        self.load_w(wkv, self.w["xa_w_kv"][i], NCH, "wkv")
        self.load_w(wq, self.w["xa_w_q"][i], NCH, "wq")
        if prefetch is not None:
            prefetch()
        kT = A.alloc("kT", [128, NCH, MEM], BF16)
        vv = A.alloc("vv", [128, 2, D], BF16)
        xt = [A.alloc("xt", [128, NCH, 512], BF16) for _ in range(2)]
        qT2 = [A.alloc("qT", [128, NCH, 512], BF16) for _ in range(2)]
        pb = [A.alloc("pb", [128, 4 * MEM], F32) for _ in range(2)]
        PT = A.alloc("PT", [128, 8, 512], BF16)
        oT = [A.alloc("oT", [128, NCH, 512], BF16) for _ in range(2)]
        sm = [A.alloc("sm", [128, 16], F32) for _ in range(2)]
        wkeys = [("wkv", c) for c in range(NCH)]
        for dc in range(NCH):
            b = dc % 2
            ps = self.pst[:, b, 0:MEM]
            for kc in range(NCH):
                P.add("pe", lambda ps=ps, kc=kc, dc=dc: nc.tensor.matmul(
                    ps, wkv[:, kc, dc * 128:(dc + 1) * 128], self.memT[:, kc, :], start=(kc == 0), stop=(kc == NCH - 1)),
                    wkeys + ["memT"], [("ps", b)])
            P.add("act", lambda ps=ps, dc=dc: nc.scalar.activation(out=kT[:, dc, :], in_=ps, func=AF.Identity),
                  [("ps", b)], ["kT"])
        for mc in range(2):
            for half in range(2):
                b = 2 + (mc * 2 + half) % 2
                ps = self.pst[:, b, :]
                for kc in range(NCH):
                    P.add("pe", lambda ps=ps, kc=kc, mc=mc, half=half: nc.tensor.matmul(
                        ps, self.memT[:, kc, mc * 128:(mc + 1) * 128], wkv[:, kc, D + half * 512:D + (half + 1) * 512],
                        start=(kc == 0), stop=(kc == NCH - 1)), wkeys + ["memT"], [("ps", b)])
                P.add("dve", lambda ps=ps, mc=mc, half=half: nc.vector.tensor_copy(
                    out=vv[:, mc, half * 512:(half + 1) * 512], in_=ps), [("ps", b)], ["vv"])
        xTv = xT_src.rearrange("(c p) t -> p c t", p=128)
        oTv = self.midT.rearrange("(c p) t -> p c t", p=128)
        scale = 256 ** -0.5
        def load_xt(tt):
            if tt < NTT:
                self.dma("sp", xt[tt % 2][:], xTv[:, :, tt * 512:(tt + 1) * 512], [], [("xt", tt % 2)])

        load_xt(0)

        def emit_q(tt, k):
            bi = tt % 2
            qT = qT2[bi]
            for dc in (2 * k, 2 * k + 1):
                b = dc % 2
                ps = self.pst[:, b, :]
                for kc in range(NCH):
                    P.add("pe", lambda ps=ps, kc=kc, dc=dc, bi=bi: nc.tensor.matmul(
                        ps, wq[:, kc, dc * 128:(dc + 1) * 128], xt[bi][:, kc, :], start=(kc == 0), stop=(kc == NCH - 1)),
                        [("wq", kc), ("xt", bi)], [("ps", b)])
                P.add("act", lambda ps=ps, dc=dc, qT=qT: nc.scalar.activation(out=qT[:, dc, :], in_=ps, func=AF.Identity, scale=scale),
                      [("ps", b)], [("qT", bi, dc)])

        def emit_rest(tt):
            bi = tt % 2
            qT = qT2[bi]
            load_xt(tt + 1)

            def qn(k):
                if tt + 1 < NTT:
                    emit_q(tt + 1, k)

            def scores(s4):
                sb_ = 2 + 2 * (s4 % 2)
                for hh in range(4):
                    ps = self.pst[:, sb_ + hh // 2, (hh % 2) * 256:(hh % 2 + 1) * 256]
                    for dl in range(2):
                        P.add("pe", lambda ps=ps, hh=hh, dl=dl, s4=s4: nc.tensor.matmul(
                            ps, qT[:, 2 * hh + dl, s4 * 128:(s4 + 1) * 128], kT[:, 2 * hh + dl, :],
                            start=(dl == 0), stop=(dl == 1)),
                            [("qT", bi, 2 * hh + dl), "kT"], [("ps", sb_ + hh // 2)])
                pss = self.pst[:, sb_:sb_ + 2, :]
                smt = sm[s4 % 2]
                pbt = pb[s4 % 2]
                sk = ("sm", s4 % 2)
                P.add("dve", lambda pss=pss, smt=smt: nc.vector.reduce_max(
                    out=smt[:, 0:4], in_=pss.rearrange("p a (h m) -> p (a h) m", h=2), axis=AX.X),
                    [("ps", sb_), ("ps", sb_ + 1)], [(sk, 0)])
                P.add("dve", lambda smt=smt: nc.vector.tensor_scalar(
                    out=smt[:, 4:8], in0=smt[:, 0:4], scalar1=-1.0, scalar2=None, op0=ALU.mult), [(sk, 0)], [(sk, 1)])
                for hh in range(4):
                    ps = self.pst[:, sb_ + hh // 2, (hh % 2) * 256:(hh % 2 + 1) * 256]
                    P.add("act", lambda ps=ps, hh=hh, smt=smt, pbt=pbt: nc.scalar.activation(
                        out=pbt[:, hh * 256:(hh + 1) * 256], in_=ps, func=AF.Exp, bias=smt[:, 4 + hh:5 + hh],
                        accum_out=smt[:, 8 + hh:9 + hh]),
                        [("ps", sb_ + hh // 2), (sk, 1)], [("pb", s4 % 2, hh), (sk, 2, hh)])
                P.add("dve", lambda smt=smt: nc.vector.reciprocal(out=smt[:, 12:16], in_=smt[:, 8:12]),
                      [(sk, 2, hh) for hh in range(4)], [(sk, 3)])
                for hh in range(4):
                    P.add("act", lambda hh=hh, smt=smt, pbt=pbt: nc.scalar.activation(
                        out=pbt[:, hh * 256:(hh + 1) * 256], in_=pbt[:, hh * 256:(hh + 1) * 256], func=AF.Identity,
                        scale=smt[:, 12 + hh:13 + hh]),
                        [("pb", s4 % 2, hh), (sk, 3)], [("pb", s4 % 2, hh)])

            def transposes(s4):
                pbt = pb[s4 % 2]
                for half in range(2):
                    b = 6 + half
                    ps = self.pst[:, b, :]
                    for j4 in range(4):
                        j = half * 4 + j4
                        P.add("pe", lambda ps=ps, j4=j4, j=j, pbt=pbt: nc.tensor.transpose(
                            ps[:, j4 * 128:(j4 + 1) * 128], pbt[:, j * 128:(j + 1) * 128], self.ident[:]),
                            [("pb", s4 % 2, j // 2), "ident"], [("ps", b)])
                    outap = PT[:, half * 4:(half + 1) * 4, s4 * 128:(s4 + 1) * 128]
                    inap = ps.rearrange("p (c t) -> p c t", c=4)
                    if half == 0:
                        P.add("act", lambda outap=outap, inap=inap: nc.scalar.activation(out=outap, in_=inap, func=AF.Identity),
                              [("ps", b)], [("PT", s4, half)])
                    else:
                        P.add("dve", lambda outap=outap, inap=inap: nc.vector.tensor_copy(out=outap, in_=inap),
                              [("ps", b)], [("PT", s4, half)])

            for s4 in range(4):
                scores(s4)
                if s4 > 0:
                    transposes(s4 - 1)
                qn(s4)
            transposes(3)
            ptk = [("PT", a, hh) for a in range(4) for hh in range(2)]
            for dc in range(NCH):
                hh = dc // 2
                b = dc % 2
                ps = self.pst[:, b, :]
                for mc in range(2):
                    P.add("pe", lambda ps=ps, mc=mc, dc=dc, hh=hh: nc.tensor.matmul(
                        ps, vv[:, mc, dc * 128:(dc + 1) * 128], PT[:, 2 * hh + mc, :], start=(mc == 0), stop=(mc == 1)),
                        ["vv"] + ptk, [("ps", b)])
                if dc % 2 == 0:
                    P.add("act", lambda ps=ps, dc=dc, bi=bi: nc.scalar.activation(out=oT[bi][:, dc, :], in_=ps, func=AF.Identity),
                          [("ps", b)], [("oT", bi, dc)])
                else:
                    P.add("dve", lambda ps=ps, dc=dc, bi=bi: nc.vector.tensor_copy(out=oT[bi][:, dc, :], in_=ps),
                          [("ps", b)], [("oT", bi, dc)])
            self.dma("sp", oTv[:, :, tt * 512:(tt + 1) * 512], oT[bi][:], [("oT", bi, dc) for dc in range(NCH)], [("oTd", tt)])

        for k in range(4):
            emit_q(0, k)
        for tt in range(NTT):
            emit_rest(tt)
        P.flush()

    def phase_ffn_a(self, i, xT_src, prefetch_chunk=None):
        P, nc, A = self.P, self.nc, self.A
        A.reset()
        A.cap = self.WRES_BASE
        NP = FF // 128
        xt = A.alloc("xt", [128, NCH, S], BF16)
        xTv = xT_src.rearrange("(c p) t -> p c t", p=128)
        for tt in range(NTT):
            self.dma("sp", xt[:, :, tt * 512:(tt + 1) * 512], xTv[:, :, tt * 512:(tt + 1) * 512], [], [("xt", tt)])
        cwraw = A.alloc("cwraw", [88, 2, 128], F32)
        cw2 = A.alloc("cw2", [128, 2, 88], F32)
        cwv = self.w["ffn_conv_w"][i]
        self.dma("sp", cwraw[:, 0, :], cwv[0:2].rearrange("k (c p) -> (k c) p", p=128), [], [("cwraw", 0)])
        self.dma("sp", cwraw[0:44, 1, :], cwv[2].rearrange("(c p) -> c p", p=128), [], [("cwraw", 1)])
        self.dma("sp", cwraw[44:88, 1, :], self.w["ffn_conv_b"][i].rearrange("(c p) -> c p", p=128), [], [("cwraw", 2)])
        for blk in range(2):
            P.add("pe", lambda blk=blk: nc.tensor.transpose(self.pst[:, 7, blk * 128:blk * 128 + 88], cwraw[:, blk, :], self.ident[0:88, 0:88]),
                  [("cwraw", 0), ("cwraw", 1), ("cwraw", 2), "ident"], [("ps", 7)])
        P.add("dve", lambda: nc.vector.tensor_copy(out=cw2[:], in_=self.pst[:, 7, 0:256].rearrange("p (a n) -> p a n", a=2)[:, :, 0:88]),
              [("ps", 7)], ["cw"])
        NPq = 2 * NP

        def cwk(k, ch):
            if k == 0:
                return cw2[:, 0, ch:ch + 1]
            if k == 1:
                return cw2[:, 0, NPq + ch:NPq + ch + 1]
            if k == 2:
                return cw2[:, 1, ch:ch + 1]
            return cw2[:, 1, NPq + ch:NPq + ch + 1]
        wu = [A.alloc("wu", [128, NCH, 256], BF16) for _ in range(2)]
        U = [A.alloc("U", [128, 2 + S], F32) for _ in range(2)]
        ac = [A.alloc("ac", [128, S], BF16) for _ in range(2)]
        for w_ in range(2):
            P.add("pool", lambda w_=w_: nc.gpsimd.memset(U[w_][:, 0:2], 0.0), [], [("U", w_, -1)])
        wv = self.w["ffn_w_up"][i].rearrange("(c p) n -> p c n", p=128)
        av = self.actT.rearrange("(c p) t -> c p t", p=128)
        cv = [[A.alloc("cv3", [128, 512], F32) for _ in range(3)] for _ in range(2)]
        steps = [(j, tt) for j in range(NP) for tt in range(NTT)]

        def load_pair(j):
            if j >= NP:
                return
            wb = j % 2
            for w_ in range(2):
                col = w_ * FF + j * 128
                self.dma("pool", wu[wb][:, :, w_ * 128:(w_ + 1) * 128], wv[:, :, col:col + 128], [], [("wu", wb, w_)])

        def stA(q):
            j, tt = steps[q]
            wb = j % 2
            r = q % 3
            if tt == 1:
                load_pair(j + 1)
            if tt == 4 and prefetch_chunk is not None:
                prefetch_chunk(j)
            for w_ in range(2):
                b = (q % 4) * 2 + w_
                ps = self.pst[:, b, :]
                ch = w_ * NP + j
                for kc in range(NCH):
                    P.add("pe", lambda ps=ps, kc=kc, wb=wb, w_=w_, tt=tt: nc.tensor.matmul(
                        ps, wu[wb][:, kc, w_ * 128:(w_ + 1) * 128], xt[:, kc, tt * 512:(tt + 1) * 512],
                        start=(kc == 0), stop=(kc == NCH - 1)),
                        [("wu", wb, w_), ("xt", tt)], [("ps", b)])
                c_ = cv[w_][r]
                ck = ("cv", w_, r)
                P.add("act", lambda ps=ps, w_=w_, tt=tt: nc.scalar.activation(
                    out=U[w_][:, 2 + tt * 512:2 + (tt + 1) * 512], in_=ps, func=AF.Identity),
                    [("ps", b)], [("U", w_, tt)])
                P.add("act", lambda ps=ps, c_=c_, ch=ch: nc.scalar.activation(
                    out=c_[:], in_=ps, func=AF.Identity, scale=cwk(2, ch), bias=cwk(3, ch)),
                    [("ps", b), "cw"], [ck])
                P.add("dve", lambda c_=c_, w_=w_, tt=tt, ch=ch: nc.vector.scalar_tensor_tensor(
                    out=c_[:], in0=U[w_][:, 1 + tt * 512:1 + (tt + 1) * 512], scalar=cwk(1, ch), in1=c_[:],
                    op0=ALU.mult, op1=ALU.add), [("U", w_, tt), ("U", w_, tt - 1), ck, "cw"], [ck])
                P.add("dve", lambda c_=c_, w_=w_, tt=tt, ch=ch: nc.vector.scalar_tensor_tensor(
                    out=c_[:], in0=U[w_][:, tt * 512:(tt + 1) * 512], scalar=cwk(0, ch), in1=c_[:],
                    op0=ALU.mult, op1=ALU.add), [("U", w_, tt), ("U", w_, tt - 1), ck, "cw"], [ck])

        def stB(q):
            j, tt = steps[q]
            wb = j % 2
            r = q % 3
            cg = cv[1][r]
            P.add("act", lambda cg=cg: nc.scalar.activation(out=cg[:], in_=cg[:], func=AF.Gelu_apprx_tanh),
                  [("cv", 1, r)], [("cv", 1, r)])
            P.add("pool", lambda r=r, wb=wb, tt=tt: nc.gpsimd.tensor_tensor(
                out=ac[wb][:, tt * 512:(tt + 1) * 512], in0=cv[0][r][:], in1=cv[1][r][:], op=ALU.mult),
                [("cv", 0, r), ("cv", 1, r)], [("ac", wb, tt)])
            if tt == NTT - 1:
                self.dma("sp", av[j], ac[wb][:], [("ac", wb, t2) for t2 in range(NTT)], [("acd", j)])

        load_pair(0)
        NQ = len(steps)
        for q in range(NQ + 1):
            if q < NQ:
                stA(q)
            if q >= 1:
                stB(q - 1)
        P.flush()

    def phase_sb(self, j, xT_src, prefetch=None):
        self.phase_sb_qkv(j, xT_src)
        self.phase_sb_attn(prefetch)

    def phase_sb_qkv(self, j, xT_src):
        P, nc, A = self.P, self.nc, self.A
        A.reset()
        A.cap = 229312
        wt = A.alloc("wqkv", [128, NCH, 3 * D], BF16)
        self.load_w(wt, self.w["sb_w_qkv"][j], NCH, "wqkv")
        xt = [A.alloc("xt", [128, NCH, 512], BF16) for _ in range(2)]
        qk = [A.alloc("qk", [128, 16, 512], BF16) for _ in range(2)]
        vt = [A.alloc("vt", [128, 4, D], BF16) for _ in range(2)]
        xTv = xT_src.rearrange("(c p) t -> p c t", p=128)
        qTv = self.qT.rearrange("(c p) t -> p c t", p=128)
        kTv = self.kT.rearrange("(c p) t -> p c t", p=128)
        vv = self.vtok.rearrange("(t s p) d -> t p s d", p=128, s=4)
        scale = 64 ** -0.5
        n = 0
        for tt in range(NTT):
            bi = tt % 2
            self.dma("sp", xt[bi][:], xTv[:, :, tt * 512:(tt + 1) * 512], [], [("xt", bi)])
            for oc in range(16):
                b = n % 4; n += 1
                ps = self.pst[:, b, :]
                for kc in range(NCH):
                    P.add("pe", lambda ps=ps, kc=kc, oc=oc, bi=bi: nc.tensor.matmul(
                        ps, wt[:, kc, oc * 128:(oc + 1) * 128], xt[bi][:, kc, :], start=(kc == 0), stop=(kc == NCH - 1)),
                        [("wqkv", kc), ("xt", bi)], [("ps", b)])
                if oc % 2 == 0:
                    P.add("act", lambda ps=ps, oc=oc, bi=bi: nc.scalar.activation(
                        out=qk[bi][:, oc, :], in_=ps, func=AF.Identity, scale=(scale if oc < 8 else 1.0)),
                        [("ps", b)], [("qk", bi, oc)])
                else:
                    P.add("dve", lambda ps=ps, oc=oc, bi=bi: nc.vector.tensor_scalar(
                        out=qk[bi][:, oc, :], in0=ps, scalar1=(scale if oc < 8 else 1.0), scalar2=None, op0=ALU.mult),
                        [("ps", b)], [("qk", bi, oc)])
            self.dma("sp", qTv[:, :, tt * 512:(tt + 1) * 512], qk[bi][:, 0:8, :], [("qk", bi, oc) for oc in range(8)], [("qTd", tt)])
            self.dma("sp", kTv[:, :, tt * 512:(tt + 1) * 512], qk[bi][:, 8:16, :], [("qk", bi, oc) for oc in range(8, 16)], [("kTd", tt)])
            for s4 in range(4):
                for half in range(2):
                    b = n % 4; n += 1
                    ps = self.pst[:, b, :]
                    for kc in range(NCH):
                        P.add("pe", lambda ps=ps, kc=kc, bi=bi, s4=s4, half=half: nc.tensor.matmul(
                            ps, xt[bi][:, kc, s4 * 128:(s4 + 1) * 128], wt[:, kc, 2 * D + half * 512:2 * D + (half + 1) * 512],
                            start=(kc == 0), stop=(kc == NCH - 1)), [("wqkv", kc), ("xt", bi)], [("ps", b)])
                    if half == 0:
                        P.add("act", lambda ps=ps, bi=bi, s4=s4, half=half: nc.scalar.activation(
                            out=vt[bi][:, s4, half * 512:(half + 1) * 512], in_=ps, func=AF.Identity),
                            [("ps", b)], [("vt", bi, s4, half)])
                    else:
                        P.add("dve", lambda ps=ps, bi=bi, s4=s4, half=half: nc.vector.tensor_copy(
                            out=vt[bi][:, s4, half * 512:(half + 1) * 512], in_=ps),
                            [("ps", b)], [("vt", bi, s4, half)])
            self.dma("sp", vv[tt], vt[bi][:], [("vt", bi, a, hh) for a in range(4) for hh in range(2)], [("vd", tt)])
        P.flush()

    def phase_sb_attn(self, prefetch=None):
        P, nc, A = self.P, self.nc, self.A
        A.reset()
        A.cap = self.WRES_BASE
        if prefetch is not None:
            prefetch()
        qTp = [A.alloc("qTp", [128, S], BF16) for _ in range(2)]
        kz = [[A.alloc("kz", [128, S], BF16) for _ in range(2)] for _ in range(2)]
        vp = [A.alloc("vp", [128, NSUB, 128], BF16) for _ in range(2)]
        oTp = [A.alloc("oTp", [128, S], BF16) for _ in range(2)]
        Ef = [A.alloc("Ef", [128, 1024], F32) for _ in range(2)]
        SPb = [A.alloc("SPb", [128, 1024], BF16) for _ in range(3)]
        Wb = [A.alloc("Wb", [128, 1024], BF16) for _ in range(3)]
        RS = A.alloc("RS", [128, 512], BF16)
        zer = A.alloc("zer", [128, 512], BF16)
        P.add("pool", lambda: nc.gpsimd.memset(zer[:], 0.0), [], ["zer"])
        for pb_ in range(2):
            for hl in range(2):
                lo = 64 * (1 - hl)
                P.add("pool", lambda pb_=pb_, hl=hl, lo=lo: nc.gpsimd.memset(kz[pb_][hl][lo:lo + 64, :], 0.0),
                      [], [("kzz", pb_, hl)])
        vv = self.vtok.rearrange("(n p) d -> p n d", p=128)
        negtri = self.tri[:, 0, :]
        negone = self.tri[:, 1, :]
        mask = self.tri[:, 2, :]
        units = []
        qcount = 0
        for c in range(NCH):
            for hl in range(2):
                for qi in range(NTT):
                    first = True
                    for r in range(3, -1, -1):
                        sc = 4 * qi + r
                        units.append(dict(c=c, hl=hl, qi=qi, ch=[sc], c0=128 * r, diag=True, first=first,
                                          last=(sc == 0), qn=qcount))
                        first = False
                    for sa in range(4 * qi - 1, 0, -2):
                        units.append(dict(c=c, hl=hl, qi=qi, ch=[sa, sa - 1], c0=0, diag=False, first=False,
                                          last=(sa - 1 == 0), qn=qcount))
                    qcount += 1
        loaded = set()

        def load_pair(c):
            if c in loaded or c >= NCH:
                return
            loaded.add(c)
            pb_ = c % 2
            self.dma("sp", qTp[pb_][:], self.qT[c * 128:(c + 1) * 128, :], [], [("qTp", pb_)])
            for hl in range(2):
                lo = 64 * hl
                self.dma("sp", kz[pb_][hl][lo:lo + 64, :], self.kT[c * 128 + lo:c * 128 + lo + 64, :], [], [("kz", pb_, hl)])
            for n0 in range(0, NSUB, 8):
                self.dma("sp", vp[pb_][:, n0:n0 + 8, :], vv[:, n0:n0 + 8, c * 128:(c + 1) * 128], [], [("vp", pb_, n0)])

        def zmm(ps, u, sc, start, stop, wkeys):
            c, hl, qi, c0 = u["c"], u["hl"], u["qi"], u["c0"]
            pb_ = c % 2
            P.add("pe", lambda: nc.tensor.matmul(ps, kz[pb_][hl][:, sc * 128:(sc + 1) * 128],
                                                 qTp[pb_][:, qi * 512 + c0:(qi + 1) * 512], start=start, stop=stop),
                  [("kz", pb_, hl), ("kzz", pb_, hl), ("qTp", pb_)], wkeys)

        def stA(n):
            u = units[n]
            load_pair(u["c"])
            c0 = u["c0"]
            nchk = len(u["ch"])
            ef = Ef[n % 2]
            sp_ = SPb[n % 3]
            if nchk == 1:
                b = n % 2
                pkeys = [("ps", b)]
                ps = self.pst[:, b, c0:512]
                zmm(ps, u, u["ch"][0], True, True, pkeys)
                efv, spv, psv = ef[:, c0:512], sp_[:, c0:512], ps
            else:
                pkeys = [("ps", 0), ("ps", 1)]
                for k_, sc in enumerate(u["ch"]):
                    zmm(self.pst[:, k_, :], u, sc, True, True, [("ps", k_)])
                efv, spv = ef[:, :], sp_[:, :]
                psv = self.pst[:, 0:2, :].rearrange("p a n -> p (a n)")
            P.add("act", lambda: nc.scalar.activation(out=efv, in_=psv, func=AF.Exp), pkeys, [("Ef", n % 2)])
            P.add("act", lambda: nc.scalar.activation(out=spv, in_=efv, func=AF.Ln, bias=1.0), [("Ef", n % 2)], [("SP", n % 3)])
            if u["diag"]:
                P.add("dve", lambda: nc.vector.tensor_tensor(out=sp_[:, c0:c0 + 128], in0=sp_[:, c0:c0 + 128], in1=mask, op=ALU.mult),
                      [("SP", n % 3), "tri"], [("SP", n % 3)])

        def stB(n):
            u = units[n]
            c0 = u["c0"]
            nchk = len(u["ch"])
            sp_ = SPb[n % 3]
            wb = Wb[n % 3]
            b0 = 2 + 2 * (n % 2)
            if u["first"]:
                P.add("dve", lambda: nc.vector.memset(RS[:], 0.0), [], ["RS"])
            if nchk == 1:
                pkeys = [("ps", b0)]
                ps = self.pst[:, b0, c0:512]
                zmm(ps, u, u["ch"][0], True, False, pkeys)
                P.add("pe", lambda: nc.tensor.matmul(ps, negtri, sp_[:, c0:512], start=False, stop=u["first"]),
                      [("SP", n % 3), "tri"], pkeys)
                if not u["first"]:
                    P.add("pe", lambda: nc.tensor.matmul(ps, negone, RS[:, c0:512], start=False, stop=True), ["RS", "tri"], pkeys)
                psv, wbv = ps, wb[:, c0:512]
            else:
                pkeys = [("ps", b0), ("ps", b0 + 1)]
                for k_, sc in enumerate(u["ch"]):
                    ps = self.pst[:, b0 + k_, :]
                    zmm(ps, u, sc, True, False, [("ps", b0 + k_)])
                    P.add("pe", lambda ps=ps, k_=k_: nc.tensor.matmul(ps, negtri, sp_[:, k_ * 512:(k_ + 1) * 512], start=False, stop=False),
                          [("SP", n % 3), "tri"], [("ps", b0 + k_)])
                    if k_ == 1:
                        P.add("pe", lambda ps=ps: nc.tensor.matmul(ps, negone, sp_[:, 0:512], start=False, stop=False),
                              [("SP", n % 3), "tri"], [("ps", b0 + k_)])
                    P.add("pe", lambda ps=ps: nc.tensor.matmul(ps, negone, RS[:, :], start=False, stop=True), ["RS", "tri"], [("ps", b0 + k_)])
                psv = self.pst[:, b0:b0 + 2, :].rearrange("p a n -> p (a n)")
                wbv = wb[:, :]
            P.add("act", lambda: nc.scalar.activation(out=wbv, in_=psv, func=AF.Exp), pkeys, [("Wb", n % 3)])
            if u["diag"]:
                P.add("dve", lambda: nc.vector.tensor_tensor(out=wb[:, c0:c0 + 128], in0=wb[:, c0:c0 + 128], in1=mask, op=ALU.mult),
                      [("Wb", n % 3), "tri"], [("Wb", n % 3)])
            if not u["last"]:
                if nchk == 1:
                    P.add("dve", lambda: nc.vector.tensor_tensor(out=RS[:, c0:512], in0=RS[:, c0:512], in1=sp_[:, c0:512], op=ALU.add),
                          ["RS", ("SP", n % 3)], ["RS"])
                else:
                    for k_ in range(2):
                        P.add("dve", lambda k_=k_: nc.vector.tensor_tensor(out=RS[:, :], in0=RS[:, :], in1=sp_[:, k_ * 512:(k_ + 1) * 512], op=ALU.add),
                              ["RS", ("SP", n % 3)], ["RS"])

        def stO(n):
            u = units[n]
            c, hl, qi, c0 = u["c"], u["hl"], u["qi"], u["c0"]
            pb_ = c % 2
            b = 6 + u["qn"] % 2
            wb = Wb[n % 3]
            if u["first"]:
                P.add("pe", lambda: nc.tensor.matmul(self.pst[:, b, :], vp[pb_][:, 0, :], zer[:], start=True, stop=False),
                      [("vp", pb_, 0), "zer"], [("ps", b)])
            nchk = len(u["ch"])
            for k_, sc in enumerate(u["ch"]):
                rhs = wb[:, c0:512] if nchk == 1 else wb[:, k_ * 512:(k_ + 1) * 512]
                P.add("pe", lambda sc=sc, rhs=rhs, k_=k_: nc.tensor.matmul(self.pst[:, b, c0:512], vp[pb_][:, sc, :], rhs, start=False,
                                                                  stop=(u["last"] and k_ == nchk - 1)),
                      [("vp", pb_, (sc // 8) * 8), ("Wb", n % 3)], [("ps", b)])
            if u["last"]:
                lo = 64 * hl
                P.add("dve", lambda: nc.vector.tensor_copy(out=oTp[pb_][lo:lo + 64, qi * 512:(qi + 1) * 512],
                                                           in_=self.pst[lo:lo + 64, b, :]),
                      [("ps", b)], [("oTp", pb_, hl, qi)])
                if hl == 1 and qi == NTT - 1:
                    self.dma("sp", self.midT[c * 128:(c + 1) * 128, :], oTp[pb_][:],
                             [("oTp", pb_, a_, q_) for a_ in range(2) for q_ in range(NTT)], [("oTd", c)])
                    load_pair(c + 2)

        load_pair(0)
        load_pair(1)
        N = len(units)
        for n in range(N + 2):
            if n < N:
                stA(n)
            if 0 <= n - 1 < N:
                stB(n - 1)
            if 0 <= n - 2 < N:
                stO(n - 2)
        P.flush()

    def phase_s5(self, j, xT_src):
        P, nc, A = self.P, self.nc, self.A
        A.reset()
        A.cap = 229312
        PI = float(np.pi)
        MUL, ADD, SUB = ALU.mult, ALU.add, ALU.subtract

        def TT(eng, out, a, b, op, rk, wk):
            h = P.h[eng]
            P.add(eng, lambda: h.tensor_tensor(out=out, in0=a, in1=b, op=op), rk, wk)

        def TS(eng, out, a, s1, s2, op0, op1, rk, wk):
            h = P.h[eng]
            if op1 is None:
                P.add(eng, lambda: h.tensor_scalar(out=out, in0=a, scalar1=s1, scalar2=None, op0=op0), rk, wk)
            else:
                P.add(eng, lambda: h.tensor_scalar(out=out, in0=a, scalar1=s1, scalar2=s2, op0=op0, op1=op1), rk, wk)

        def ACTF(out, a, func, rk, wk, **kw):
            P.add("act", lambda: nc.scalar.activation(out=out, in_=a, func=func, **kw), rk, wk)

        self._bank = 0

        def nb():
            b = self._bank
            self._bank = (b + 1) % 8
            return b

        cs5 = A.alloc("cs5", [128, 10, 128], F32)
        self.dma("sp", cs5[:], self.c_s5, [], ["cs5"])
        II = cs5[0:64, 0, :]
        bd = cs5[:, 1, :]
        aR = A.alloc("aR", [64, 64], F32); aI = A.alloc("aI", [64, 64], F32); ls = A.alloc("ls", [64, 64], F32)
        T1 = A.alloc("T1", [64, 64], F32); T2 = A.alloc("T2", [64, 64], F32)
        T3 = A.alloc("T3", [64, 64], F32); T4 = A.alloc("T4", [64, 64], F32)
        fr = A.alloc("fr", [64, 64], F32); fi = A.alloc("fi", [64, 64], F32)
        L = A.alloc("L", [64, 9, 2, 64], F32)
        off_AK = A.off
        AK = A.alloc("AK", [64, 10, 3, 64], F32)
        A.off = max(A.off, off_AK + 8192)
        AKp = A.alloc("AKp", [128, 10, 3, 32], F32)
        off_Bt = A.off
        Bt = A.alloc("Bt", [64, 2, 64, 16], F32)
        Bb = A.alloc("Bb", [64, 2, 64, 16], F32)
        U1 = A.alloc("U1", [64, 1024], F32); U2 = A.alloc("U2", [64, 1024], F32)
        off_Cn = A.off
        Cnat = A.alloc("Cnat", [128, 2, 8, 64], F32)
        A.off = max(A.off, off_Cn + 8192)
        CT = A.alloc("CT", [64, 3, 1024], F32)
        dT = A.alloc("dT", [128, 8], F32)
        araw = A.alloc("araw", [64, 2, 64], F32)
        draw = A.alloc("draw", [8, 128], F32)
        self.dma("sp", araw[:, 0, :], self.w["s5_a_re"][j], [], [("araw", 0)])
        self.dma("sp", araw[:, 1, :], self.w["s5_a_im"][j], [], [("araw", 1)])
        self.dma("sp", draw[:], self.w["s5_d"][j].rearrange("(go q) -> go q", q=128), [], ["draw"])
        for ri, dst_ in enumerate((aR, aI)):
            P.add("pe", lambda ri=ri: nc.tensor.transpose(self.pst[0:64, 7, ri * 64:(ri + 1) * 64], araw[:, ri, :], self.ident[0:64, 0:64]),
                  [("araw", ri), "ident"], [("ps", 7)])
        P.add("pe", lambda: nc.tensor.transpose(self.pst[:, 7, 128:136], draw[:], self.ident[0:8, 0:8]), ["draw", "ident"], [("ps", 7)])
        P.add("dve", lambda: nc.vector.tensor_copy(out=aR[:], in_=self.pst[0:64, 7, 0:64]), [("ps", 7)], ["aR"])
        P.add("dve", lambda: nc.vector.tensor_copy(out=aI[:], in_=self.pst[0:64, 7, 64:128]), [("ps", 7)], ["aI"])
        P.add("dve", lambda: nc.vector.tensor_copy(out=dT[:], in_=self.pst[:, 7, 128:136]), [("ps", 7)], ["dT"])
        self.dma("sp", ls[:], self.w["s5_log_step"][j].partition_broadcast(64), [], ["ls"])
        for g0 in range(0, 64, 16):
            self.dma("sp", Bt[:, 0, g0:g0 + 16, :], self.w["s5_b_re"][j][g0:g0 + 16].rearrange("g p c -> p g c"), [], [("Bt0", g0)])
            self.dma("sp", Bt[:, 1, g0:g0 + 16, :], self.w["s5_b_im"][j][g0:g0 + 16].rearrange("g p c -> p g c"), [], [("Bt1", g0)])
        self.dma("sp", Cnat[:, 0], self.w["s5_c_re"][j].rearrange("(go gl) c p -> (gl c) go p", go=8), [], ["Cn0"])
        self.dma("sp", Cnat[:, 1], self.w["s5_c_im"][j].rearrange("(go gl) c p -> (gl c) go p", go=8), [], ["Cn1"])
        ACTF(ls[:], ls[:], AF.Exp, ["ls"], ["ls"])
        TT("dve", T1[:], aR[:], ls[:], MUL, ["aR", "ls"], ["T1"])
        ACTF(T1[:], T1[:], AF.Exp, ["T1"], ["T1"])
        TT("dve", T2[:], aI[:], ls[:], MUL, ["aI", "ls"], ["T2"])
        KI = A.alloc("KI", [64, 64], mybir.dt.int32)
        TS("dve", T3[:], T2[:], 1.0 / (2 * PI), None, MUL, None, ["T2"], ["T3"])
        P.add("dve", lambda: nc.vector.tensor_copy(out=KI[:], in_=T3[:]), ["T3"], ["KI"])
        P.add("dve", lambda: nc.vector.tensor_copy(out=T3[:], in_=KI[:]), ["KI"], ["T3"])
        P.add("dve", lambda: nc.vector.scalar_tensor_tensor(out=T2[:], in0=T3[:], scalar=-2 * PI, in1=T2[:], op0=MUL, op1=ADD),
              ["T3", "T2"], ["T2"])
        ACTF(T3[:], T2[:], AF.Sin, ["T2"], ["T3"], scale=0.5)
        ACTF(T4[:], T2[:], AF.Sin, ["T2"], ["T4"], scale=-0.5, bias=PI / 2)
        TT("dve", fr[:], T3[:], T4[:], MUL, ["T3", "T4"], ["fr"])
        TT("dve", T3[:], T3[:], T3[:], MUL, ["T3"], ["T3"])
        TT("dve", T4[:], T4[:], T4[:], MUL, ["T4"], ["T4"])
        TT("dve", T4[:], T4[:], T3[:], SUB, ["T4", "T3"], ["T4"])
        TS("dve", T3[:], fr[:], 2.0, None, MUL, None, ["fr"], ["T3"])
        lr = L[:, 1, 0, :]; li = L[:, 1, 1, :]
        TT("dve", lr, T1[:], T4[:], MUL, ["T1", "T4"], [("L", 1, 0)])
        TT("dve", li, T1[:], T3[:], MUL, ["T1", "T3"], [("L", 1, 1)])
        P.add("pool", lambda: nc.gpsimd.memset(L[:, 0, 0, :], 1.0), [], [("L", 0, 0)])
        P.add("pool", lambda: nc.gpsimd.memset(L[:, 0, 1, :], 0.0), [], [("L", 0, 1)])
        for k in range(1, 8):
            kr, ki = L[:, k, 0, :], L[:, k, 1, :]
            TT("dve", T1[:], kr, lr, MUL, [("L", k, 0), ("L", 1, 0)], ["T1"])
            TT("dve", T2[:], ki, li, MUL, [("L", k, 1), ("L", 1, 1)], ["T2"])
            TT("dve", L[:, k + 1, 0, :], T1[:], T2[:], SUB, ["T1", "T2"], [("L", k + 1, 0)])
            TT("dve", T3[:], kr, li, MUL, [("L", k, 0), ("L", 1, 1)], ["T3"])
            TT("dve", T4[:], ki, lr, MUL, [("L", k, 1), ("L", 1, 0)], ["T4"])
            TT("dve", L[:, k + 1, 1, :], T3[:], T4[:], ADD, ["T3", "T4"], [("L", k + 1, 1)])
        P.add("dve", lambda: nc.vector.tensor_copy(out=AK[:, 0, 0, :], in_=L[:, 8, 0, :]), [("L", 8, 0)], [("AK", 0)])
        P.add("dve", lambda: nc.vector.tensor_copy(out=AK[:, 0, 1, :], in_=L[:, 8, 1, :]), [("L", 8, 1)], [("AK", 0)])
        for l in range(9):
            ar_, ai_ = AK[:, l, 0, :], AK[:, l, 1, :]
            TT("dve", T1[:], ar_, ar_, MUL, [("AK", l)], ["T1"])
            TT("dve", T2[:], ai_, ai_, MUL, [("AK", l)], ["T2"])
            TT("dve", AK[:, l + 1, 0, :], T1[:], T2[:], SUB, ["T1", "T2"], [("AK", l + 1)])
            TT("dve", T3[:], ar_, ai_, MUL, [("AK", l)], ["T3"])
            TS("dve", AK[:, l + 1, 1, :], T3[:], 2.0, None, MUL, None, ["T3"], [("AK", l + 1)])
        akk = [("AK", l) for l in range(10)]
        TS("dve", AK[:, :, 2, :], AK[:, :, 1, :], -1.0, None, MUL, None, akk, ["AKn"])
        for l0 in range(0, 10, 2):
            b = nb()
            ps = self.pst[:, b, 0:384]
            P.add("pe", lambda ps=ps, l0=l0: nc.tensor.matmul(ps, II, AK[:, l0:l0 + 2, :, :].rearrange("p l c g -> p (l c g)"),
                                                              start=True, stop=True), akk + ["AKn", "cs5"], [("ps", b)])
            for gp in range(2):
                src_ = self.pst[gp * 64:(gp + 1) * 64, b, 0:384].rearrange("p (l c j two) -> p l c j two", l=2, c=3, two=2)[:, :, :, :, gp]
                P.add("dve", lambda src_=src_, gp=gp, l0=l0: nc.vector.tensor_copy(out=AKp[gp * 64:(gp + 1) * 64, l0:l0 + 2, :, :], in_=src_),
                      [("ps", b)], [("AKp", l0, gp)])
        akpk = [("AKp", l0, gp) for l0 in range(0, 10, 2) for gp in range(2)]
        TT("dve", T1[:], aR[:], aR[:], MUL, ["aR"], ["T1"])
        TT("dve", T2[:], aI[:], aI[:], MUL, ["aI"], ["T2"])
        TT("dve", T1[:], T1[:], T2[:], ADD, ["T1", "T2"], ["T1"])
        P.add("dve", lambda: nc.vector.reciprocal(out=T1[:], in_=T1[:]), ["T1"], ["T1"])
        TS("dve", T2[:], lr, -1.0, None, ADD, None, [("L", 1, 0)], ["T2"])
        TT("dve", T3[:], T2[:], aR[:], MUL, ["T2", "aR"], ["T3"])
        TT("dve", T4[:], li, aI[:], MUL, [("L", 1, 1), "aI"], ["T4"])
        TT("dve", T3[:], T3[:], T4[:], ADD, ["T3", "T4"], ["T3"])
        TT("dve", fr[:], T3[:], T1[:], MUL, ["T3", "T1"], ["fr"])
        TT("dve", T3[:], li, aR[:], MUL, [("L", 1, 1), "aR"], ["T3"])
        TT("dve", T4[:], T2[:], aI[:], MUL, ["T2", "aI"], ["T4"])
        TT("dve", T3[:], T3[:], T4[:], SUB, ["T3", "T4"], ["T3"])
        TT("dve", fi[:], T3[:], T1[:], MUL, ["T3", "T1"], ["fi"])
        frb = fr[:, :].unsqueeze(2).broadcast_to([64, 64, 16])
        fib = fi[:, :].unsqueeze(2).broadcast_to([64, 64, 16])
        U1v = U1[:, :].rearrange("p (g c) -> p g c", c=16)
        U2v = U2[:, :].rearrange("p (g c) -> p g c", c=16)
        TT("dve", U1v, Bt[:, 0], frb, MUL, [("Bt0", g0) for g0 in range(0, 64, 16)] + ["fr"], ["U1"])
        TT("dve", U2v, Bt[:, 1], fib, MUL, [("Bt1", g0) for g0 in range(0, 64, 16)] + ["fi"], ["U2"])
        TT("dve", Bb[:, 0], U1v, U2v, SUB, ["U1", "U2"], ["Bb0"])
        TT("dve", U1v, Bt[:, 1], frb, MUL, [("Bt1", g0) for g0 in range(0, 64, 16)] + ["fr"], ["U1"])
        TT("dve", U2v, Bt[:, 0], fib, MUL, [("Bt0", g0) for g0 in range(0, 64, 16)] + ["fi"], ["U2"])
        TT("dve", Bb[:, 1], U1v, U2v, ADD, ["U1", "U2"], ["Bb1"])
        for ri in range(2):
            for hb_ in range(2):
                b = nb()
                for g4 in range(4):
                    go = hb_ * 4 + g4
                    P.add("pe", lambda b=b, g4=g4, go=go, ri=ri: nc.tensor.transpose(
                        self.pst[0:64, b, g4 * 128:(g4 + 1) * 128], Cnat[:, ri, go, :], self.ident[:]),
                        [f"Cn{ri}", "ident"], [("ps", b)])
                P.add("dve", lambda b=b, ri=ri, hb_=hb_: nc.vector.tensor_copy(
                    out=CT[:, ri, hb_ * 512:(hb_ + 1) * 512], in_=self.pst[0:64, b, :]), [("ps", b)], [("CT", ri, hb_)])
        TS("dve", CT[:, 2, :], CT[:, 1, :], -1.0, None, MUL, None, [("CT", 1, 0), ("CT", 1, 1)], [("CT", 2)])
        ctk = [("CT", 0, 0), ("CT", 0, 1), ("CT", 1, 0), ("CT", 1, 1), ("CT", 2)]
        Lk = [("L", k, c) for k in range(9) for c in range(2)]
        X = nc.alloc_sbuf_tensor_at("X_al", [64, 2, 8, 128], F32, offset=off_Bt)
        G = nc.alloc_sbuf_tensor_at("G_al", [64, 2, 8, 128], F32, offset=off_AK)
        Wd2 = [A.alloc("Wd", [128, 8, 128], BF16) for _ in range(2)]
        tmpW = A.alloc("tmpW", [128, 128], F32)
        WzT = A.alloc("WzT", [128, 4, 8, 2, 128], BF16)
        Gy2 = [A.alloc("Gy", [128, 4, 8, 2, 128], BF16) for _ in range(2)]
        wi = [A.alloc("wi", [128, NCH, 128], BF16) for _ in range(2)]
        xt = [A.alloc("xt", [128, NCH, 256], BF16) for _ in range(2)]
        uT2 = [A.alloc("uT", [128, 8, 512], BF16) for _ in range(2)]
        ZA2 = [[[A.alloc("ZA", [128, 512], F32) for _ in range(2)] for _ in range(4)] for _ in range(2)]
        SS = [[nc.alloc_sbuf_tensor_at(f"SS_al{a_}{b_}", [128, 512], F32, offset=off_Cn + (a_ * 2 + b_) * 2048)
               for b_ in range(2)] for a_ in range(2)]
        Hb = [[A.alloc("Hb", [128, 512], BF16) for _ in range(2)] for _ in range(4)]
        yT = A.alloc("yT", [128, S], BF16)
        for jl in range(4):
            for ri in range(2):
                P.add("pool", lambda jl=jl, ri=ri: nc.gpsimd.memset(Hb[jl][ri][:, 0:1], 0.0), [], [("Hb0", jl, ri)])
        xTv = xT_src.rearrange("(c p) t -> p c t", p=128)
        wv = self.w["s5_w_in"][j].rearrange("(c p) n -> p c n", p=128)
        V1 = U1[:, :].rearrange("p (a g c) -> p a g c", a=8, c=16)
        V2 = U2[:, :].rearrange("p (a g c) -> p a g c", a=8, c=16)
        nxt = [0]
        first_ss = [True]

        def part1(go):
            pg = go % 2
            Wd, Gy, uT, ZA = Wd2[pg], Gy2[pg], uT2[pg], ZA2[pg]
            oc0 = go * 128
            gs = slice(go * 8, (go + 1) * 8)
            bsh = [64, 8, 8, 16]
            LRb = L[:, 0:8, 0, gs].unsqueeze(3).broadcast_to(bsh)
            LIb = L[:, 0:8, 1, gs].unsqueeze(3).broadcast_to(bsh)
            L1Rb = L[:, 1:9, 0, gs].unsqueeze(3).broadcast_to(bsh)
            L1Ib = L[:, 1:9, 1, gs].unsqueeze(3).broadcast_to(bsh)
            Bbr = Bb[:, 0, gs, :].unsqueeze(1).broadcast_to(bsh)
            Bbi = Bb[:, 1, gs, :].unsqueeze(1).broadcast_to(bsh)
            CTr = CT[:, 0, oc0:oc0 + 128].rearrange("p (g c) -> p g c", c=16).unsqueeze(1).broadcast_to(bsh)
            CTi = CT[:, 1, oc0:oc0 + 128].rearrange("p (g c) -> p g c", c=16).unsqueeze(1).broadcast_to(bsh)
            X0 = X[:, 0].rearrange("p a (g c) -> p a g c", c=16)
            X1 = X[:, 1].rearrange("p a (g c) -> p a g c", c=16)
            G0 = G[:, 0].rearrange("p a (g c) -> p a g c", c=16)
            G1 = G[:, 1].rearrange("p a (g c) -> p a g c", c=16)
            TT("dve", V1, LRb, Bbr, MUL, Lk + ["Bb0"], ["U1"])
            TT("dve", V2, LIb, Bbi, MUL, Lk + ["Bb1"], ["U2"])
            TT("dve", X0, V1, V2, SUB, ["U1", "U2"], ["X0"] + [("Bt0", g0) for g0 in range(0, 64, 16)] + [("Bt1", g0) for g0 in range(0, 64, 16)])
            TT("dve", V1, LRb, Bbi, MUL, Lk + ["Bb1"], ["U1"])
            TT("dve", V2, LIb, Bbr, MUL, Lk + ["Bb0"], ["U2"])
            TT("dve", X1, V1, V2, ADD, ["U1", "U2"], ["X1"] + [("Bt0", g0) for g0 in range(0, 64, 16)] + [("Bt1", g0) for g0 in range(0, 64, 16)])
            TT("dve", V1, L1Rb, CTr, MUL, Lk + ctk, ["U1"])
            TT("dve", V2, L1Ib, CTi, MUL, Lk + ctk, ["U2"])
            TT("dve", G0, V1, V2, SUB, ["U1", "U2"], ["G0"] + akk + ["AKn"])
            TT("dve", V1, L1Ib, CTr, MUL, Lk + ctk, ["U1"])
            TT("dve", V2, L1Rb, CTi, MUL, Lk + ctk, ["U2"])
            TT("dve", V1, V1, V2, ADD, ["U1", "U2"], ["U1"])
            TS("dve", G1, V1, -1.0, None, MUL, None, ["U1"], ["G1"] + akk + ["AKn"])
            for half in range(2):
                b = nb()
                for d4 in range(4):
                    dl = half * 4 + d4
                    ps = self.pst[:, b, d4 * 128:(d4 + 1) * 128]
                    P.add("pe", lambda ps=ps, dl=dl, oc0=oc0: nc.tensor.matmul(ps, X[:, 0, dl, :], CT[:, 0, oc0:oc0 + 128], start=True, stop=False),
                          ["X0"] + ctk, [("ps", b)])
                    P.add("pe", lambda ps=ps, dl=dl, oc0=oc0: nc.tensor.matmul(ps, X[:, 1, dl, :], CT[:, 2, oc0:oc0 + 128], start=False, stop=True),
                          ["X1"] + ctk, [("ps", b)])
                psv = self.pst[:, b, :].rearrange("p (a n) -> p a n", a=4)
                TT("dve", Wd[:, half * 4:(half + 1) * 4, :], psv, bd.unsqueeze(1).broadcast_to([128, 4, 128]), MUL,
                   [("ps", b), "cs5"], [("Wd%d" % pg, half)])
                if half == 0:
                    TT("dve", tmpW[:], self.pst[:, b, 0:128], bd, MUL, [("ps", b), "cs5"], ["tmpW"])
                    P.add("dve", lambda go=go: nc.vector.scalar_tensor_tensor(
                        out=Wd[:, 0, :], in0=self.ident[:], scalar=dT[:, go:go + 1], in1=tmpW[:], op0=MUL, op1=ADD),
                        ["tmpW", "ident", "dT", ("Wd%d" % pg, 0)], [("Wd%d" % pg, 0)])
            for bk in range(4):
                b = nb()
                for i4 in range(4):
                    idx = bk * 4 + i4
                    s_, ri = divmod(idx, 2)
                    ps = self.pst[:, b, i4 * 128:(i4 + 1) * 128]
                    P.add("pe", lambda ps=ps, s_=s_, ri=ri: nc.tensor.matmul(ps, X[:, ri, 7 - s_, :], II, start=True, stop=True),
                          ["X0", "X1", "cs5"], [("ps", b)])
                psv = self.pst[:, b, :].rearrange("p (a n) -> p a n", a=4)
                for jl in range(4):
                    TT("dve", WzT[:, jl, 2 * bk:2 * bk + 2, :, :].rearrange("p a b n -> p (a b) n"), psv,
                       cs5[:, 2 + jl, :].unsqueeze(1).broadcast_to([128, 4, 128]), MUL, [("ps", b), "cs5"], [("WzT", jl, bk)])
            for bk in range(4):
                b = nb()
                for i4 in range(4):
                    idx = bk * 4 + i4
                    t_, ri = divmod(idx, 2)
                    ps = self.pst[:, b, i4 * 128:(i4 + 1) * 128]
                    P.add("pe", lambda ps=ps, t_=t_, ri=ri: nc.tensor.matmul(ps, II, G[:, ri, t_, :], start=True, stop=True),
                          ["G0", "G1", "cs5"], [("ps", b)])
                psv = self.pst[:, b, :].rearrange("p (a n) -> p a n", a=4)
                for jl in range(4):
                    TT("dve", Gy[:, jl, 2 * bk:2 * bk + 2, :, :].rearrange("p a b n -> p (a b) n"), psv,
                       cs5[:, 6 + jl, :].unsqueeze(1).broadcast_to([128, 4, 128]), MUL, [("ps", b), "cs5"], [("Gy%d" % pg, jl, bk)])
            wb = go % 2
            self.dma("pool", wi[wb][:], wv[:, :, oc0:oc0 + 128], [], [("wi", wb)])
            for th in range(2 * NTT):
                xb = nxt[0] % 2; nxt[0] += 1
                self.dma("sp", xt[xb][:], xTv[:, :, th * 256:(th + 1) * 256], [], [("xt", xb)])
                b = nb()
                ps = self.pst[:, b, 0:256]
                for kc in range(NCH):
                    P.add("pe", lambda ps=ps, kc=kc, wb=wb, xb=xb: nc.tensor.matmul(
                        ps, wi[wb][:, kc, :], xt[xb][:, kc, :], start=(kc == 0), stop=(kc == NCH - 1)),
                        [("wi", wb), ("xt", xb)], [("ps", b)])
                ACTF(uT[:, :, th * 32:(th + 1) * 32], ps.rearrange("p (b s) -> p s b", s=8), AF.Identity, [("ps", b)], [("uT%d" % pg, th // 2, th % 2)])
            utk = [("uT%d" % pg, tt, hh_) for tt in range(NTT) for hh_ in range(2)]
            for jl in range(4):
                for ri in range(2):
                    b = nb()
                    ps = self.pst[:, b, :]
                    for s_ in range(8):
                        P.add("pe", lambda ps=ps, jl=jl, s_=s_, ri=ri: nc.tensor.matmul(
                            ps, WzT[:, jl, s_, ri, :], uT[:, s_, :], start=(s_ == 0), stop=(s_ == 7)),
                            utk + [("WzT", jl, bk) for bk in range(4)], [("ps", b)])
                    ACTF(ZA[jl][ri][:], ps, AF.Identity, [("ps", b)], [("ZA%d" % pg, jl, ri)])
        def part2(go):
            pg = go % 2
            Wd, Gy, uT, ZA = Wd2[pg], Gy2[pg], uT2[pg], ZA2[pg]
            for jl in range(4):
                jg = go * 4 + jl
                si = jl % 2
                for l in range(9):
                    d = 1 << l
                    if l % 2 == 0:
                        src_, dst_, sk, dk = ZA[jl], SS[si], ("ZA%d" % pg, jl), ("SS", si)
                    else:
                        src_, dst_, sk, dk = SS[si], ZA[jl], ("SS", si), ("ZA%d" % pg, jl)
                    ar_ = AKp[:, l, 0, jg:jg + 1]; ai_ = AKp[:, l, 1, jg:jg + 1]; nai_ = AKp[:, l, 2, jg:jg + 1]
                    n_ = 512 - d

                    def stt(out, in0, sc, in1, rk, wk):
                        P.add("dve", lambda: nc.vector.scalar_tensor_tensor(out=out, in0=in0, scalar=sc, in1=in1, op0=MUL, op1=ADD), rk, wk)
                    cnk = ["Cn0", "Cn1"] if first_ss[0] else []
                    first_ss[0] = False if l >= 1 else first_ss[0]
                    stt(dst_[0][:, d:512], src_[0][:, 0:n_], ar_, src_[0][:, d:512], [sk + (0,)] + akpk, [dk + (0,)] + cnk)
                    stt(dst_[0][:, d:512], src_[1][:, 0:n_], nai_, dst_[0][:, d:512], [sk + (1,), dk + (0,)] + akpk, [dk + (0,)])
                    stt(dst_[1][:, d:512], src_[1][:, 0:n_], ar_, src_[1][:, d:512], [sk + (1,)] + akpk, [dk + (1,)] + cnk)
                    stt(dst_[1][:, d:512], src_[0][:, 0:n_], ai_, dst_[1][:, d:512], [sk + (0,), dk + (1,)] + akpk, [dk + (1,)])
                    for ri in range(2):
                        P.add("pool", lambda ri=ri, d=d, src_=src_, dst_=dst_: nc.gpsimd.tensor_copy(out=dst_[ri][:, 0:d], in_=src_[ri][:, 0:d]),
                              [sk + (ri,)], [dk + (ri,)] + cnk)
                for ri in range(2):
                    ACTF(Hb[jl][ri][:, 1:512], SS[si][ri][:, 0:511], AF.Identity, [("SS", si, ri)], [("Hb", jl, ri)])
        def part3(go):
            pg = go % 2
            Wd, Gy, uT, ZA = Wd2[pg], Gy2[pg], uT2[pg], ZA2[pg]
            oc0 = go * 128
            utk = [("uT%d" % pg, tt, hh_) for tt in range(NTT) for hh_ in range(2)]
            yv = yT[:, :].rearrange("p (b t) -> p t b", t=8)
            for t_ in range(8):
                b = nb()
                ps = self.pst[:, b, :]
                mms = [(Wd[:, t_ - s_, :], uT[:, s_, :]) for s_ in range(t_ + 1)]
                mms += [(Gy[:, jl, t_, ri, :], Hb[jl][ri][:]) for jl in range(4) for ri in range(2)]
                rk = utk + [("Wd%d" % pg, 0), ("Wd%d" % pg, 1)] + [("Gy%d" % pg, jl, bk) for jl in range(4) for bk in range(4)] + \
                    [("Hb", jl, ri) for jl in range(4) for ri in range(2)] + [("Hb0", jl, ri) for jl in range(4) for ri in range(2)]
                for mi, (lt, rh) in enumerate(mms):
                    P.add("pe", lambda ps=ps, lt=lt, rh=rh, mi=mi, nm_=len(mms): nc.tensor.matmul(
                        ps, lt, rh, start=(mi == 0), stop=(mi == nm_ - 1)), rk, [("ps", b)])
                ACTF(yv[:, t_, :], ps, AF.Gelu_apprx_tanh, [("ps", b)], [("yT", t_)])
            self.dma("sp", self.midT[oc0:oc0 + 128, :], yT[:], [("yT", t_) for t_ in range(8)], [("yTd", go)])
        part1(0)
        for go in range(8):
            if go + 1 < 8:
                part1(go + 1)
            part2(go)
            part3(go)
        P.flush()

    def build(self):
        plan = self.plan
        cur = 0
        self.phase_prep(self.xT[cur])
        x_src = self.x_in
        n = len(plan)
        for idx, (kind, i) in enumerate(plan):
            last = idx == n - 1
            x_dst = self.y if last else self.xs
            xT_dst = None if last else self.xT[1 - cur]
            lg = self.w["ln_g"]
            lb = self.w["ln_b"]
            if kind == "xa":
                Wp = self.wres_tile(NCH, D)
                self.phase_xa(i, self.xT[cur], prefetch=lambda: self.load_w(Wp, self.w["xa_w_o"][i], NCH, "Wpre"))
                self.phase_proj_ln(self.midT, NCH, None, False, lg[i, 1], lb[i, 1], x_src, x_dst, xT_dst, Wpre=Wp)
            elif kind == "ffn":
                Wp = self.wres_tile(FF // 128, D)
                wdv = self.w["ffn_w_down"][i].rearrange("(c p) n -> p c n", p=128)
                self.phase_ffn_a(i, self.xT[cur], prefetch_chunk=lambda c, Wp=Wp, wdv=wdv: self.dma(
                    "pool", Wp[:, c, :], wdv[:, c, :], [], [("Wpre", c)]))
                self.phase_proj_ln(self.actT, FF // 128, None, False, lg[i, 2], lb[i, 2], x_src, x_dst, xT_dst, Wpre=Wp)
            elif kind == "s5":
                self.phase_s5(i // 2, self.xT[cur])
                self.phase_proj_ln(self.midT, NCH, self.w["s5_w_out"][i // 2], True, lg[i, 0], lb[i, 0], x_src, x_dst, xT_dst)
            elif kind == "sb":
                Wp = self.wres_tile(NCH, D)
                self.phase_sb(i // 2, self.xT[cur], prefetch=lambda: self.load_w(Wp, self.w["sb_w_o"][i // 2], NCH, "Wpre"))
                self.phase_proj_ln(self.midT, NCH, None, False, lg[i, 0], lb[i, 0], x_src, x_dst, xT_dst, Wpre=Wp)
            x_src = self.xs
            cur = 1 - cur
        self.P.finish()
        return self.nc


def make_consts():
    ident = np.eye(128, dtype=np.float32)
    tri = np.zeros((128, 4, 128), np.float32)
    j = np.arange(128)[:, None]
    s = np.arange(128)[None, :]
    tri[:, 0, :] = -(j >= s).astype(np.float32)
    tri[:, 1, :] = -1.0
    tri[:, 2, :] = (j < s).astype(np.float32)
    cs5 = np.zeros((128, 10, 128), np.float32)
    q = np.arange(128)
    cs5[:64, 0, :64] = np.eye(64); cs5[:64, 0, 64:] = np.eye(64)
    cs5[:, 1, :] = (q[:, None] // 16 == q[None, :] // 16)
    for jl in range(4):
        cs5[:, 2 + jl, :] = (q[:, None] // 16 == 2 * jl + q[None, :] // 64)
        cs5[:, 6 + jl, :] = (q[None, :] // 16 == 2 * jl + q[:, None] // 64)
    return ident, tri, cs5


FULL_PLAN = []
for _i in range(DEPTH):
    FULL_PLAN += [("s5" if _i % 2 == 0 else "sb", _i), ("xa", _i), ("ffn", _i)]

_CACHE = {}


def run_plan(plan, inputs, n_cores=8):
    key = tuple(plan)
    if key not in _CACHE:
        bld = Builder(list(plan))
        _CACHE[key] = (bld.build(), list(bld.w.keys()))
    nc, used = _CACHE[key]
    ident, tri, cs5 = make_consts()
    x = np.ascontiguousarray(inputs["x"])
    mem = np.ascontiguousarray(inputs["mem"])
    in_maps = []
    for c in range(n_cores):
        m = {k: np.ascontiguousarray(inputs[k]) for k in used}
        m["x"] = x[c]
        m["mem"] = mem[c]
        m["c_ident"] = ident
        m["c_tri"] = tri
        m["c_s5"] = cs5
        in_maps.append(m)
    res = run_bass_kernel_spmd(nc, in_maps, core_ids=list(range(n_cores)))
    return np.stack([np.asarray(r["y"]) for r in res.results], axis=0)


def kernel(**inputs):
    out = run_plan(FULL_PLAN, inputs, 8)
    return out.astype(np.float32)
```
